# Optimizing a Trainium2 kernel written in Bass

```python
import math
import jax, jax.numpy as jnp
from jax import lax
import numpy as np

D_MODEL = 2048
BATCH = 16
SEQ = 256
DEPTH = 4
DEC_BATCH = 2
DEC_SEQ = 1024
PAST_LEN = 256

GRID_W = 64
POS_BASE = 10000.0
NORM_EPS = 1e-6
SSD_HEADS = 32
SSD_HEAD_DIM = 64
SSD_INNER = SSD_HEADS * SSD_HEAD_DIM
SSD_GROUPS = 4
SSD_STATE = 128
SSD_CONV = 4
SSD_CHUNK = 128
SSD_XBC = SSD_INNER + 2 * SSD_GROUPS * SSD_STATE
SC_WIDTH = 1024
SC_CONV = 3
FT_WIDTH = 1024
FT_GROUPS = 4
FT_GROUP_DIM = FT_WIDTH // FT_GROUPS
D_FF = 5504
N_BRANCH = 3
N_MOD = 9
IN_SIZES = (SSD_INNER, SSD_XBC, 2 * SSD_HEADS, SC_WIDTH, SC_WIDTH, SC_WIDTH, FT_WIDTH, N_BRANCH * D_MODEL)
IN_COLS = SSD_INNER + SSD_XBC + 2 * SSD_HEADS + 3 * SC_WIDTH + FT_WIDTH + N_BRANCH * D_MODEL

kernel_name = "hybrid_ssd_shortconv_fnet_diffusion_step"


def rms_norm(x, g):
    xf = x.astype(jnp.float32)
    y = xf * lax.rsqrt(jnp.mean(xf * xf, axis=-1, keepdims=True) + NORM_EPS)
    return (y * g.astype(jnp.float32)).astype(x.dtype)


def modulate(x, shift, scale):
    return x * (1.0 + scale[:, None, :]) + shift[:, None, :]


def dwconv(x, w):
    K = w.shape[0]
    L = x.shape[1]
    left = (K - 1) // 2
    xp = jnp.pad(x, ((0, 0), (left, K - 1 - left), (0, 0)))
    out = xp[:, 0:L] * w[0]
    for k in range(1, K):
        out = out + xp[:, k:k + L] * w[k]
    return out


def swiglu(x, w_gu, w_d):
    g, u = jnp.split(x @ w_gu, 2, axis=-1)
    return (jax.nn.silu(g) * u) @ w_d


def segsum(a):
    T = a.shape[-1]
    cs = jnp.cumsum(a, axis=-1)
    diff = cs[..., :, None] - cs[..., None, :]
    mask = jnp.tril(jnp.ones((T, T), dtype=bool))
    return jnp.where(mask, diff, -jnp.inf)


def ssd_scan(x, dt, a_neg, b, c, h0):
    bt, L, H, P = x.shape
    Q = SSD_CHUNK
    nc = L // Q
    G = SSD_GROUPS
    hg = H // G
    xd = (x * dt[..., None]).reshape(bt, nc, Q, G, hg, P)
    a = jnp.moveaxis((dt * a_neg).reshape(bt, nc, Q, G, hg), 2, -1)
    a_cs = jnp.cumsum(a, axis=-1)
    bc = b.reshape(bt, nc, Q, G, SSD_STATE)
    cc = c.reshape(bt, nc, Q, G, SSD_STATE)
    decay_in = jnp.exp(segsum(a))
    y_diag = jnp.einsum("bclgn,bcsgn,bcgjls,bcsgjp->bclgjp", cc, bc, decay_in, xd)
    decay_to_end = jnp.exp(a_cs[..., -1:] - a_cs)
    chunk_states = jnp.einsum("bcsgn,bcgjs,bcsgjp->bcgjpn", bc, decay_to_end, xd)
    chunk_decay = jnp.exp(a_cs[..., -1])

    def step(h, inp):
        s, d = inp
        return h * d[..., None, None] + s, h

    h0g = h0.reshape(bt, G, hg, P, SSD_STATE)
    h_fin, h_prev = lax.scan(step, h0g, (jnp.moveaxis(chunk_states, 1, 0), jnp.moveaxis(chunk_decay, 1, 0)))
    h_prev = jnp.moveaxis(h_prev, 0, 1)
    y_off = jnp.einsum("bclgn,bcgjpn,bcgjl->bclgjp", cc, h_prev, jnp.exp(a_cs))
    y = (y_diag + y_off).reshape(bt, L, H, P)
    return y, h_fin.reshape(bt, H, P, SSD_STATE)


def ssd_mixer(z, xbc_raw, dt_raw, conv_w, conv_b, dt_bias, a_log, d_skip, norm_g, h0):
    bt, L, _ = z.shape
    f32 = jnp.float32
    xbc = jax.nn.silu(dwconv(xbc_raw, conv_w) + conv_b).astype(f32)
    xs = xbc[..., :SSD_INNER].reshape(bt, L, SSD_HEADS, SSD_HEAD_DIM)
    bm = xbc[..., SSD_INNER:SSD_INNER + SSD_GROUPS * SSD_STATE].reshape(bt, L, SSD_GROUPS, SSD_STATE)
    cm = xbc[..., SSD_INNER + SSD_GROUPS * SSD_STATE:].reshape(bt, L, SSD_GROUPS, SSD_STATE)
    dt = jax.nn.softplus(dt_raw.astype(f32).reshape(bt, L, 2, SSD_HEADS) + dt_bias.astype(f32))
    a_neg = -jnp.exp(a_log.astype(f32))
    h0 = h0.astype(f32)
    y_f, h_f = ssd_scan(xs, dt[:, :, 0], a_neg[0], bm, cm, h0[:, 0])
    flip = lambda t: jnp.flip(t, axis=1)
    y_b, h_b = ssd_scan(flip(xs), flip(dt[:, :, 1]), a_neg[1], flip(bm), flip(cm), h0[:, 1])
    y = y_f + flip(y_b) + xs * d_skip.astype(f32)[:, None]
    y = y.reshape(bt, L, SSD_INNER) * jax.nn.silu(z.astype(f32))
    y = rms_norm(y, norm_g)
    return y.astype(z.dtype), jnp.stack([h_f, h_b], axis=1)


def fourier_mixer(u):
    bt, L, _ = u.shape
    uf = u.astype(jnp.float32).reshape(bt, L, FT_GROUPS, FT_GROUP_DIM)
    y = jnp.fft.fft2(uf, axes=(1, 3), norm="ortho").real
    return y.reshape(bt, L, FT_WIDTH).astype(u.dtype)


def grid_pos_emb(n_tok):
    rows = n_tok // GRID_W
    t = jnp.arange(rows * GRID_W)
    r = (t // GRID_W).astype(jnp.float32)[:, None]
    col = (t % GRID_W).astype(jnp.float32)[:, None]
    nf = D_MODEL // 4
    omega = 1.0 / (POS_BASE ** (jnp.arange(nf, dtype=jnp.float32) / nf))
    return jnp.concatenate([jnp.sin(r * omega), jnp.cos(r * omega), jnp.sin(col * omega), jnp.cos(col * omega)], axis=-1)


def trunk_layer(x, mod, h0, lp):
    bt, L, D = x.shape
    sh1, sc1, g1, sh2, sc2, g2, sh3, sc3, g3 = jnp.split(mod.astype(x.dtype), N_MOD, axis=-1)
    ng = lp["norm_g"]
    u = modulate(rms_norm(x, ng[0]), sh1, sc1)
    x = x + 0.5 * g1[:, None, :] * rms_norm(swiglu(u, lp["ffn1_wgu"], lp["ffn1_wd"]), ng[1])
    u = modulate(rms_norm(x, ng[2]), sh2, sc2)
    proj = u @ lp["w_in"]
    offs = [int(o) for o in np.cumsum(IN_SIZES)[:-1]]
    z, xbc, dt_raw, sc_b, sc_c, sc_x, ft_in, gate_raw = jnp.split(proj, offs, axis=-1)
    y_ssd, h_fin = ssd_mixer(z, xbc, dt_raw, lp["ssd_conv_w"], lp["ssd_conv_b"], lp["ssd_dt_bias"],
                             lp["ssd_a_log"], lp["ssd_d"], lp["ssd_norm_g"], h0)
    y_sc = sc_b * dwconv(sc_c * sc_x, lp["sc_conv_w"])
    y_ft = fourier_mixer(ft_in)
    gates = jax.nn.sigmoid(gate_raw.astype(jnp.float32)).astype(x.dtype).reshape(bt, L, N_BRANCH, D)
    merged = (gates[:, :, 0] * (y_ssd @ lp["w_br_ssd"])
              + gates[:, :, 1] * (y_sc @ lp["w_br_sc"])
              + gates[:, :, 2] * (y_ft @ lp["w_br_ft"]))
    x = x + g2[:, None, :] * rms_norm(merged @ lp["w_out"], ng[3])
    u = modulate(rms_norm(x, ng[4]), sh3, sc3)
    x = x + 0.5 * g3[:, None, :] * rms_norm(swiglu(u, lp["ffn2_wgu"], lp["ffn2_wd"]), ng[5])
    return x, h_fin


def setup_inputs(seed: int = 0) -> dict:
    key = jax.random.key(seed)
    ks = iter(jax.random.split(key, 32))
    f32 = jnp.float32
    D = D_MODEL

    def nrm(shape, scale):
        return jax.random.normal(next(ks), shape, f32) * scale

    x_prompt = nrm((BATCH, SEQ, D), 1.0)
    x_sample = nrm((DEC_BATCH, DEC_SEQ, D), 1.0)
    state_ssd = nrm((DEC_BATCH, DEPTH, 2, SSD_HEADS, SSD_HEAD_DIM, SSD_STATE), 0.1)
    c = nrm((DEC_BATCH, D), 1.0)
    c_ctx = nrm((D,), 1.0)
    ada_w = nrm((DEPTH, D, N_MOD * D), 0.5 * D ** -0.5)
    ada_b = nrm((DEPTH, N_MOD * D), 0.02)
    norm_g = 1.0 + nrm((DEPTH, 6, D), 0.02)
    ffn1_wgu = nrm((DEPTH, D, 2 * D_FF), D ** -0.5)
    ffn1_wd = nrm((DEPTH, D_FF, D), D_FF ** -0.5)
    w_in = nrm((DEPTH, D, IN_COLS), D ** -0.5)
    ssd_conv_w = nrm((DEPTH, SSD_CONV, SSD_XBC), SSD_CONV ** -0.5)
    ssd_conv_b = nrm((DEPTH, SSD_XBC), 0.02)
    dt0 = jnp.exp(jax.random.uniform(next(ks), (DEPTH, 2, SSD_HEADS), f32, math.log(1e-3), math.log(1e-1)))
    ssd_dt_bias = dt0 + jnp.log(-jnp.expm1(-dt0))
    ssd_a_log = jnp.log(jax.random.uniform(next(ks), (DEPTH, 2, SSD_HEADS), f32, 1.0, 16.0))
    ssd_d = 1.0 + nrm((DEPTH, SSD_HEADS), 0.1)
    ssd_norm_g = 1.0 + nrm((DEPTH, SSD_INNER), 0.02)
    sc_conv_w = nrm((DEPTH, SC_CONV, SC_WIDTH), SC_CONV ** -0.5)
    w_br_ssd = nrm((DEPTH, SSD_INNER, D), SSD_INNER ** -0.5)
    w_br_sc = nrm((DEPTH, SC_WIDTH, D), SC_WIDTH ** -0.5)
    w_br_ft = nrm((DEPTH, FT_WIDTH, D), FT_WIDTH ** -0.5)
    w_out = nrm((DEPTH, D, D), D ** -0.5)
    ffn2_wgu = nrm((DEPTH, D, 2 * D_FF), D ** -0.5)
    ffn2_wd = nrm((DEPTH, D_FF, D), D_FF ** -0.5)
    return {
        "x_prompt": x_prompt, "x_sample": x_sample, "state_ssd": state_ssd, "c": c, "c_ctx": c_ctx,
        "ada_w": ada_w, "ada_b": ada_b, "norm_g": norm_g, "ffn1_wgu": ffn1_wgu, "ffn1_wd": ffn1_wd,
        "w_in": w_in, "ssd_conv_w": ssd_conv_w, "ssd_conv_b": ssd_conv_b, "ssd_dt_bias": ssd_dt_bias,
        "ssd_a_log": ssd_a_log, "ssd_d": ssd_d, "ssd_norm_g": ssd_norm_g, "sc_conv_w": sc_conv_w,
        "w_br_ssd": w_br_ssd, "w_br_sc": w_br_sc, "w_br_ft": w_br_ft, "w_out": w_out,
        "ffn2_wgu": ffn2_wgu, "ffn2_wd": ffn2_wd,
    }


def reference(x_prompt, x_sample, state_ssd, c, c_ctx, ada_w, ada_b, norm_g, ffn1_wgu, ffn1_wd, w_in,
              ssd_conv_w, ssd_conv_b, ssd_dt_bias, ssd_a_log, ssd_d, ssd_norm_g, sc_conv_w,
              w_br_ssd, w_br_sc, w_br_ft, w_out, ffn2_wgu, ffn2_wd):
    n_ctx = x_prompt.shape[0]
    h_zero = jnp.zeros((n_ctx, 2, SSD_HEADS, SSD_HEAD_DIM, SSD_STATE), jnp.float32)
    x_ctx = x_prompt
    x_lat = x_sample + grid_pos_emb(x_sample.shape[1]).astype(x_sample.dtype)[None]
    ctx_states = []
    for l in range(DEPTH):
        lp = {
            "norm_g": norm_g[l], "ffn1_wgu": ffn1_wgu[l], "ffn1_wd": ffn1_wd[l], "w_in": w_in[l],
            "ssd_conv_w": ssd_conv_w[l], "ssd_conv_b": ssd_conv_b[l], "ssd_dt_bias": ssd_dt_bias[l],
            "ssd_a_log": ssd_a_log[l], "ssd_d": ssd_d[l], "ssd_norm_g": ssd_norm_g[l],
            "sc_conv_w": sc_conv_w[l], "w_br_ssd": w_br_ssd[l], "w_br_sc": w_br_sc[l],
            "w_br_ft": w_br_ft[l], "w_out": w_out[l], "ffn2_wgu": ffn2_wgu[l], "ffn2_wd": ffn2_wd[l],
        }
        mod_ctx = (jax.nn.silu(c_ctx) @ ada_w[l] + ada_b[l])[None, :]
        mod_lat = jax.nn.silu(c) @ ada_w[l] + ada_b[l]
        x_ctx, h_ctx = trunk_layer(x_ctx, mod_ctx, h_zero, lp)
        ctx_states.append(h_ctx)
        x_lat, _ = trunk_layer(x_lat, mod_lat, state_ssd[:, l], lp)
    new_state_ssd = jnp.stack(ctx_states, axis=1)
    return (x_ctx, x_lat, new_state_ssd)
```

```python
import math
import numpy as np
import concourse.bass as bass
import concourse.mybir as mybir
from concourse.bass_utils import run_bass_kernel_spmd

F32 = mybir.dt.float32
BF16 = mybir.dt.bfloat16
ALU = mybir.AluOpType
AF = mybir.ActivationFunctionType

import os as _os
SAME_ENGINE_SYNC = _os.environ.get("KSYNC", "0") == "1"
NORM_EPS = 1e-6


class Cfg:
    def __init__(self, **kw):
        self.D = 2048; self.T = 1024; self.SEG = 256; self.DEPTH = 4
        self.H = 32; self.P = 64; self.G = 4; self.N = 128; self.Q = 128
        self.SCW = 1024; self.FTW = 1024; self.FTG = 4; self.F = 5504
        self.HB = 4
        for k, v in kw.items():
            setattr(self, k, v)
        c = self
        c.nD = c.D // 128; c.HP = c.H * c.P; c.nHP = c.HP // 128
        c.XBC = c.HP + 2 * c.G * c.N; c.nXBC = c.XBC // 128
        c.nSC = c.SCW // 128; c.nFT = c.FTW // 128; c.FTGD = c.FTW // c.FTG
        c.nF = c.F // 128; c.NTC = c.T // 128; c.NTT = c.T // 512; c.NSEG = c.T // c.SEG
        c.INC = c.HP + c.XBC + 2 * c.H + 3 * c.SCW + c.FTW + 3 * c.D
        c.o_z = 0; c.o_xbc = c.HP; c.o_dt = c.HP + c.XBC; c.o_scb = c.o_dt + 2 * c.H
        c.o_scc = c.o_scb + c.SCW; c.o_scx = c.o_scc + c.SCW; c.o_ft = c.o_scx + c.SCW
        c.o_gate = c.o_ft + c.FTW
        c.hg = c.H // c.G
        assert c.FTGD == 256 and c.P == 64 and c.N == 128 and c.SEG == 256


class Op:
    __slots__ = ("eng", "fn", "deps", "dma", "dsem", "dval", "sig", "need_sig")

    def __init__(self, eng, fn, deps, dma):
        self.eng = eng; self.fn = fn; self.deps = deps; self.dma = dma
        self.dsem = None; self.dval = 0; self.sig = 0; self.need_sig = False


class Prog:
    NSEM = {"sp": 24, "pool": 8}

    def __init__(self):
        self.ops = []
        self.lastw = {}
        self.readers = {}
        self.dsem_rr = {"sp": 0, "pool": 0}
        self.dsem_cnt = {}

    def add(self, eng, fn, R=(), W=(), dma=False):
        idx = len(self.ops)
        deps = set()
        R = list(R); W = list(W)
        W += [k for k in R if k[0] == "ps"]
        R = [k for k in R if k[0] != "ps"]
        if dma:
            j = self.dsem_rr[eng]
            self.dsem_rr[eng] = (j + 1) % self.NSEM[eng]
            skey = ("dsem", eng, j)
            W.append(skey)
        for k in R:
            if k in self.lastw:
                deps.add(self.lastw[k])
        for k in W:
            if k in self.lastw:
                deps.add(self.lastw[k])
            deps |= set(self.readers.get(k, {}).values())
        for k in W:
            self.lastw[k] = idx
            self.readers[k] = {}
        for k in R:
            if k not in W:
                rk = (eng, idx) if dma else eng
                self.readers.setdefault(k, {})[rk] = idx
        deps.discard(idx)
        op = Op(eng, fn, deps, dma)
        if dma:
            c = self.dsem_cnt.get((eng, j), 0) + 1
            self.dsem_cnt[(eng, j)] = c
            op.dsem = (eng, j); op.dval = 16 * c
        self.ops.append(op)
        return idx

    def emit(self, nc, sems, dsems, block):
        ops = self.ops
        for o in ops:
            for d in o.deps:
                p = ops[d]
                if p.dma:
                    continue
                if p.eng == o.eng and (p.eng == "pe" or (not SAME_ENGINE_SYNC and p.eng in ("act", "dve"))):
                    continue
                p.need_sig = True
        cnt = {}
        for o in ops:
            if o.need_sig and not o.dma:
                cnt[o.eng] = cnt.get(o.eng, 0) + 1
                o.sig = cnt[o.eng]
        waited = {}
        plans = {e: [] for e in ("pe", "act", "dve", "pool", "sp")}
        for o in ops:
            need = {}
            for d in o.deps:
                p = ops[d]
                if p.dma:
                    key = ("d",) + p.dsem; val = p.dval
                else:
                    if p.eng == o.eng and (p.eng == "pe" or (not SAME_ENGINE_SYNC and p.eng in ("act", "dve"))):
                        continue
                    key = ("e", p.eng); val = p.sig
                if val > need.get(key, 0):
                    need[key] = val
            wl = []
            for key, val in need.items():
                if waited.get((o.eng, key), 0) >= val:
                    continue
                waited[(o.eng, key)] = val
                wl.append((key, val))
            plans[o.eng].append((o, wl))

        semv = {}
        heads = {e: 0 for e in plans}
        total = sum(len(v) for v in plans.values())
        done = 0
        while done < total:
            prog_made = False
            for en, pl in plans.items():
                while heads[en] < len(pl):
                    o, wl = pl[heads[en]]
                    if all(semv.get(key, 0) >= val for key, val in wl):
                        if o.dma:
                            semv[("d",) + o.dsem] = semv.get(("d",) + o.dsem, 0) + 16
                        elif o.need_sig:
                            semv[("e", o.eng)] = semv.get(("e", o.eng), 0) + 1
                        heads[en] += 1; done += 1; prog_made = True
                    else:
                        break
            if not prog_made:
                raise RuntimeError("DEADLOCK in sync plan: " + str({en: (heads[en], plans[en][heads[en]][1] if heads[en] < len(plans[en]) else None) for en in plans}))
        self.stats = {en: len(pl) for en, pl in plans.items()}
        self.nwaits = {en: sum(len(wl) for _, wl in pl) for en, pl in plans.items()}
        print("PROG ops per engine", self.stats, "waits", self.nwaits, "final sems", {k: v for k, v in semv.items() if k[0] == "e"})

        def run(engname, e):
            for o, wl in plans[engname]:
                for key, val in wl:
                    sem = dsems[key[1:]] if key[0] == "d" else sems[key[1]]
                    e.wait_ge(sem, val)
                ins = o.fn(e)
                if o.dma:
                    ins.then_inc(dsems[o.dsem], 16)
                elif o.need_sig:
                    ins.then_inc(sems[o.eng], 1)

        @block.tensor
        def _(e):
            run("pe", e)

        @block.scalar
        def _(e):
            run("act", e)

        @block.vector
        def _(e):
            run("dve", e)

        @block.gpsimd
        def _(e):
            run("pool", e)

        @block.sync
        def _(e):
            run("sp", e)


def build(cfg):
    c = cfg
    D, T, L = c.D, c.T, c.DEPTH
    nD, NTT, NTC = c.nD, c.NTT, c.NTC
    nc = bass.Bass("TRN2", target_bir_lowering=False)

    def din(name, shape):
        return nc.dram_tensor(name, list(shape), F32, kind="ExternalInput").ap()

    xin = din("xin", [T, D]); pos = din("pos", [T, D]); cv = din("cv", [128, nD])
    keep_d = din("keep", [128, 1]); h0 = din("h0", [L, 2, c.HP, c.N])
    ada_w = din("ada_w", [L, D, 9 * D]); ada_b = din("ada_b", [L, 128, 9 * nD])
    ng_d = din("norm_g", [L, 128, 6 * nD])
    wgu = [din("ffn1_wgu", [L, D, 2 * c.F]), din("ffn2_wgu", [L, D, 2 * c.F])]
    wd = [din("ffn1_wd", [L, c.F, D]), din("ffn2_wd", [L, c.F, D])]
    w_in = din("w_in", [L, D, c.INC])
    w_br = [din("w_br_ssd", [L, c.HP, D]), din("w_br_sc", [L, c.SCW, D]), din("w_br_ft", [L, c.FTW, D])]
    w_out = din("w_out", [L, D, D])
    convw_d = din("conv_w", [L, 128, c.nXBC * 4]); convb_d = din("conv_b", [L, 128, c.nXBC])
    dtb_d = din("dtb", [L, 128, 2 * c.H]); alog_d = din("alog", [L, 128, 2 * c.H])
    dsk_d = din("dsk", [L, 128, c.nHP]); sng_d = din("sng", [L, 128, c.nHP])
    scw_d = din("scw", [L, 128, c.nSC * 3])
    cs_d = din("dft_cs", [2, 128, 512]); cl_d = din("dft_cl", [T, T]); sl_d = din("dft_sl", [T, T])
    yout = nc.dram_tensor("yout", [T, D], F32, kind="ExternalOutput").ap()
    hout = nc.dram_tensor("hout", [L, c.NSEG, 2, c.HP, c.N], F32, kind="ExternalOutput").ap()

    pg = Prog()
    NSLOT = max(c.nF, 43) if (T >= 1024 and c.D >= 2048) else 80
    SB = T * 2
    import contextlib
    es = contextlib.ExitStack()
    with es:
        def sb(name, shape, dt):
            return es.enter_context(nc.sbuf_tensor("sb_" + name, list(shape), dt))

        X = sb("X", [128, nD, T], F32)
        U = sb("U", [128, nD, T], BF16)
        HA = sb("HA", [128, NSLOT * T], BF16)
        WR = sb("WR", [128, 2, 8 * 256], BF16)
        ident_f = sb("ident_f", [128, 128], F32)
        ident_b = sb("ident_b", [128, 128], BF16)
        ones_f = sb("ones_f", [128, 128], F32)
        mk_f = sb("mk_f", [128, 128], F32)
        mk_b = sb("mk_b", [128, 128], F32)
        modt = sb("modt", [128, 2, 9 * nD], F32)
        ngt = sb("ngt", [128, 6 * nD], F32)
        coefA = sb("coefA", [128, 3 * nD], F32)
        coefR = sb("coefR", [128, 3 * nD], F32)
        convw = sb("convw", [128, c.nXBC * 4], F32)
        convb = sb("convb", [128, c.nXBC], F32)
        dtb = sb("dtb", [128, 2 * c.H], F32)
        aneg = sb("aneg", [128, 2 * c.H], F32)
        dsk = sb("dsk", [128, c.nHP], F32)
        sng = sb("sng", [128, c.nHP], F32)
        scw = sb("scw", [128, c.nSC * 3], F32)
        keep = sb("keep", [128, 1], F32)
        scv = sb("scv", [128, nD], BF16)
        cvt = sb("cvt", [128, nD], F32)
        cst = sb("cst", [128, 2, 512], BF16)
        acc = sb("acc", [128, T], F32)
        rstd = acc
        tmpA = sb("tmpA", [128, 2, 512], F32)
        PS = [es.enter_context(nc.psum_tensor("ps%d" % b, [128, 512], F32)) for b in range(8)]
        sems = {e: es.enter_context(nc.semaphore("s_" + e)) for e in ("pe", "act", "dve", "pool", "sp")}
        dsems = {}
        for q, n in Prog.NSEM.items():
            for j in range(n):
                dsems[(q, j)] = es.enter_context(nc.semaphore("d_%s%d" % (q, j)))
        block = es.enter_context(nc.Block())

        def hview(off_b, shape, dt):
            esz = 4 if dt == F32 else 2
            n = int(np.prod(shape[1:]))
            e0 = off_b // 2
            v = HA[:, e0:e0 + n * esz // 2]
            if dt == F32:
                v = v.bitcast(F32)
            if len(shape) == 3:
                v = v.rearrange("p (a b) -> p a b", a=shape[1])
            elif len(shape) == 4:
                v = v.rearrange("p (a b c) -> p a b c", a=shape[1], b=shape[2])
            keys = [("H", g) for g in range(off_b // 1024, (off_b + n * esz + 1023) // 1024)]
            return v, keys

        class Al:
            def __init__(self, start_slot):
                self.off = start_slot * SB

            def get(self, shape, dt):
                esz = 4 if dt == F32 else 2
                n = int(np.prod(shape[1:])) * esz
                n = (n + 1023) // 1024 * 1024
                v, k = hview(self.off, shape, dt)
                self.off += n
                assert self.off <= NSLOT * SB, "HA arena overflow"
                return v, k

        def K(name, *idx):
            return (name,) + idx

        def XK(dc, tt=None):
            return [("X", dc, t) for t in (range(NTT) if tt is None else [tt])]

        def UK(dc, tt=None):
            return [("U", dc, t) for t in (range(NTT) if tt is None else [tt])]

        def ts(tt):
            return slice(tt * 512, (tt + 1) * 512)

        tmp_rr = [0]

        def tmpf():
            i = tmp_rr[0]; tmp_rr[0] ^= 1
            return tmpA[:, i, :], [("tmpA", i)]

        ACT = lambda fn, R, W: pg.add("act", fn, R, W)
        DVE = lambda fn, R, W: pg.add("dve", fn, R, W)
        PE = lambda fn, R, W: pg.add("pe", fn, R, W)

        def dma(q, out, in_, R, W):
            eng = q
            return pg.add(eng, lambda e: e.dma_start(out=out, in_=in_), R, W, dma=True)

        pg.add("pool", lambda e: e.memset(ident_f[:], 0.0), [], [K("ident_f")])
        pg.add("pool", lambda e: e.affine_select(out=ident_f[:], in_=ident_f[:], pattern=[[-1, 128]],
                                                 compare_op=ALU.not_equal, fill=1.0, base=0, channel_multiplier=1),
               [], [K("ident_f")])
        pg.add("pool", lambda e: e.memset(ones_f[:], 1.0), [], [K("ones_f")])
        pg.add("pool", lambda e: e.affine_select(out=mk_f[:], in_=ones_f[:], pattern=[[1, 128]],
                                                 compare_op=ALU.is_ge, fill=0.0, base=0, channel_multiplier=-1),
               [K("ones_f")], [K("mk_f")])
        pg.add("pool", lambda e: e.affine_select(out=mk_b[:], in_=ones_f[:], pattern=[[-1, 128]],
                                                 compare_op=ALU.is_ge, fill=0.0, base=0, channel_multiplier=1),
               [K("ones_f")], [K("mk_b")])
        DVE(lambda e: e.tensor_copy(out=ident_b[:], in_=ident_f[:]), [K("ident_f")], [K("ident_b")])
        dma("sp", keep[:], keep_d[:, :], [], [K("keep")])
        dma("sp", cvt[:], cv[:, :], [], [K("cvt")])
        ACT(lambda e: e.activation(out=scv[:], in_=cvt[:], func=AF.Silu), [K("cvt")], [K("scv")])
        dma("pool", cst[:], cs_d.rearrange("j p n -> p j n"), [], [K("cst")])

        st = {"half": 0, "wslot": 0, "bg": None}

        def gemm(Wd, l, K_, col0, ncols, rhs, evac, ntok=T, tick=False, fixed_base=None):
            nkc = K_ // 128
            nch = (ncols + 127) // 128
            ntt = max(1, ntok // 512)
            nw = min(512, ntok)
            if fixed_base is None:
                base = st["half"] * 4; st["half"] ^= 1
            else:
                base = fixed_base
            for t0 in range(0, nkc, 8):
                kn = min(8, nkc - t0)
                slot = st["wslot"]; st["wslot"] = (slot + 1) % 2
                wv = WR[:, slot, :].rearrange("p (a b) -> p a b", a=8)
                src = Wd[l, t0 * 128:(t0 + kn) * 128, col0:col0 + ncols].rearrange("(kc p) n -> p kc n", p=128)
                dma("pool", wv[:, 0:kn, 0:ncols], src, [], [K("WR", slot)])
                for ci in range(nch):
                    cw = min(128, ncols - ci * 128)
                    for kk in range(kn):
                        kc = t0 + kk
                        rap, rkeys = rhs(kc)
                        for tt in range(ntt):
                            b = base + ci * ntt + tt
                            PE(lambda e, b=b, cw=cw, wv=wv, kk=kk, ci=ci, rap=rap, tt=tt, kc=kc:
                               e.matmul(PS[b][0:cw, 0:nw], lhsT=wv[:, kk, ci * 128:ci * 128 + cw],
                                        rhs=rap[:, tt * 512:tt * 512 + nw], start=(kc == 0), stop=(kc == nkc - 1)),
                               [K("WR", slot), rkeys[tt]], [K("ps", b)])
            for ci in range(nch):
                cw = min(128, ncols - ci * 128)
                for tt in range(ntt):
                    b = base + ci * ntt + tt
                    evac(ci, tt, PS[b][0:cw, 0:nw], K("ps", b))
            if tick and st["bg"] is not None:
                try:
                    next(st["bg"])
                except StopIteration:
                    st["bg"] = None

        def bg_tick():
            if st["bg"] is not None:
                try:
                    next(st["bg"])
                except StopIteration:
                    st["bg"] = None

        def rhsU(kc):
            return U[:, kc, :], [("U", kc, t) for t in range(NTT)]

        def ada_gen(l):
            par = l % 2
            for j0 in range(0, 9 * nD, 2):
                def evac(ci, tt, ps, pk, j0=j0):
                    j = j0 + ci
                    DVE(lambda e: e.tensor_tensor(out=modt[:, par, j:j + 1], in0=ps[:, 0:1], in1=adab_t[:, par, j:j + 1],
                                                  op=ALU.add), [pk, K("adab", par)], [K("modt", par, j)])
                gemm(ada_w, l, D, j0 * 128, 256, lambda kc: (scv[:, kc:kc + 1], [K("scv")]), evac, ntok=1, tick=False, fixed_base=6)
                yield

        adab_t = sb("adab_t", [128, 2, 9 * nD], F32)

        def ada_start(l):
            par = l % 2
            dma("sp", adab_t[:, par, :], ada_b[l, :, :], [], [K("adab", par)])
            return ada_gen(l)

        def MODK(par, j0, n):
            return [K("modt", par, j) for j in range(j0, j0 + n)]

        stg, stgk = hview(0, [128, 4, D], F32)
        stg2, stg2k = hview(4 * D * 4, [128, 4, D], F32)
        for tt in range(NTT):
            for j in range(4):
                tc = tt * 4 + j
                dma("sp", stg[:, j, :], xin[tc * 128:(tc + 1) * 128, :], [], [K("stg", j)] + stgk)
                dma("sp", stg2[:, j, :], pos[tc * 128:(tc + 1) * 128, :], [], [K("stg2", j)] + stg2k)
                DVE(lambda e, j=j: e.tensor_tensor(out=stg[:, j, :], in0=stg[:, j, :], in1=stg2[:, j, :], op=ALU.add),
                    [K("stg", j), K("stg2", j)] + stgk + stg2k, [K("stg", j)] + stgk)
            for dc in range(nD):
                b = dc % 8
                for j in range(4):
                    PE(lambda e, b=b, j=j, dc=dc: e.matmul(PS[b][:, j * 128:(j + 1) * 128],
                                                           lhsT=stg[:, j, dc * 128:(dc + 1) * 128], rhs=ident_f[:],
                                                           start=True, stop=True),
                       [K("stg", j), K("ident_f")] + stgk, [K("ps", b)])
                ACT(lambda e, b=b, dc=dc, tt=tt: e.activation(out=X[:, dc, ts(tt)], in_=PS[b][:, :], func=AF.Copy),
                    [K("ps", b)], XK(dc, tt))

        import os
        if int(os.environ.get("KSTAGE", "99")) >= 1:
            for _ in ada_start(0):
                pass

        def sumsq_finish(bank0=6):
            for tt in range(NTT):
                b = bank0 + tt
                PE(lambda e, b=b, tt=tt: e.matmul(PS[b][:, :], lhsT=ones_f[:], rhs=acc[:, ts(tt)], start=True, stop=True),
                   [K("ones_f"), K("acc", tt)], [K("ps", b)])
                DVE(lambda e, b=b, tt=tt: e.tensor_scalar(out=rstd[:, ts(tt)], in0=PS[b][:, :], scalar1=float(D * NORM_EPS),
                                                          scalar2=None, op0=ALU.add),
                    [K("ps", b)], [K("acc", tt)])
                ACT(lambda e, tt=tt: e.activation(out=rstd[:, ts(tt)], in_=rstd[:, ts(tt)], func=AF.Sqrt), [K("acc", tt)], [K("acc", tt)])
                DVE(lambda e, tt=tt: e.reciprocal(out=rstd[:, ts(tt)], in_=rstd[:, ts(tt)]), [K("acc", tt)], [K("acc", tt)])

        def acc_add(first, src_ap, src_keys, tt):
            if first:
                ACT(lambda e: e.activation(out=acc[:, ts(tt)], in_=src_ap, func=AF.Square), src_keys, [K("acc", tt)])
            else:
                t, tk = tmpf()
                ACT(lambda e: e.activation(out=t, in_=src_ap, func=AF.Square), src_keys, tk)
                DVE(lambda e: e.tensor_tensor(out=acc[:, ts(tt)], in0=acc[:, ts(tt)], in1=t, op=ALU.add),
                    tk + [K("acc", tt)], [K("acc", tt)])

        def prenorm(si):
            par = st["par"]
            for dc in range(nD):
                for tt in range(NTT):
                    acc_add(dc == 0, X[:, dc, ts(tt)], XK(dc, tt), tt)
            sumsq_finish()
            for dc in range(nD):
                for tt in range(NTT):
                    t, tk = tmpf()
                    DVE(lambda e, dc=dc, tt=tt, t=t: e.tensor_tensor(out=t, in0=X[:, dc, ts(tt)], in1=rstd[:, ts(tt)], op=ALU.mult),
                        XK(dc, tt) + [K("acc", tt)], tk)
                    ACT(lambda e, dc=dc, tt=tt, t=t: e.activation(out=U[:, dc, ts(tt)], in_=t, func=AF.Identity,
                                                                  scale=coefA[:, si * nD + dc:si * nD + dc + 1],
                                                                  bias=modt[:, par, (3 * si) * nD + dc:(3 * si) * nD + dc + 1]),
                        tk + [K("coefA"), K("modt", par, 3 * si * nD + dc)], UK(dc, tt))

        def post_update(si):
            sumsq_finish()
            for dc in range(nD):
                for tt in range(NTT):
                    t, tk = tmpf()
                    DVE(lambda e, dc=dc, tt=tt, t=t: e.tensor_tensor(out=t, in0=U[:, dc, ts(tt)], in1=rstd[:, ts(tt)], op=ALU.mult),
                        UK(dc, tt) + [K("acc", tt)], tk)
                    DVE(lambda e, dc=dc, tt=tt, t=t: e.scalar_tensor_tensor(out=X[:, dc, ts(tt)], in0=t,
                                                                            scalar=coefR[:, si * nD + dc:si * nD + dc + 1],
                                                                            in1=X[:, dc, ts(tt)], op0=ALU.mult, op1=ALU.add),
                        tk + [K("coefR")] + XK(dc, tt), XK(dc, tt))

        def out_evac_factory(first_flag):
            def mk(dc0):
                def evac(ci, tt, ps, pk):
                    dc = dc0 + ci
                    acc_add(first_flag["v"] and ci == 0 and dc0 == 0, ps, [pk], tt)
                    DVE(lambda e: e.tensor_copy(out=U[:, dc, ts(tt)], in_=ps), [pk], UK(dc, tt))
                return evac
            return mk

        def ffn(l, fi, si):
            import os
            KSUB = int(os.environ.get("KSUB", "99"))
            prenorm(si)
            if KSUB < 1:
                return
            nF = c.nF
            Hc = [hview(i * SB, [128, T], BF16) for i in range(nF)]
            for c0 in range(0, nF, 2):
                ncol = min(256, (nF - c0) * 128)

                def evac_g(ci, tt, ps, pk, c0=c0):
                    hv, hk = Hc[c0 + ci]
                    ACT(lambda e: e.activation(out=hv[:, ts(tt)], in_=ps, func=AF.Silu), [pk], [hk[tt]])

                def evac_u(ci, tt, ps, pk, c0=c0):
                    hv, hk = Hc[c0 + ci]
                    DVE(lambda e: e.tensor_tensor(out=hv[:, ts(tt)], in0=ps, in1=hv[:, ts(tt)], op=ALU.mult),
                        [pk, hk[tt]], [hk[tt]])
                gemm(wgu[fi], l, D, c0 * 128, ncol, rhsU, evac_g)
                gemm(wgu[fi], l, D, c.F + c0 * 128, ncol, rhsU, evac_u)
            if KSUB < 2:
                return
            ff = {"v": True}
            mk = out_evac_factory(ff)
            for d0 in range(0, nD, 2):
                gemm(wd[fi], l, c.F, d0 * 128, min(256, (nD - d0) * 128), lambda kc: (Hc[kc][0], Hc[kc][1]), mk(d0))
                ff["v"] = False
            if KSUB < 3:
                return
            post_update(si)

        def layer_consts(l):
            par = l % 2
            st["par"] = par
            for (t, d, nm) in ((ngt, ng_d, "ngt"), (convw, convw_d, "convw"), (convb, convb_d, "convb"), (dtb, dtb_d, "dtb"),
                               (aneg, alog_d, "aneg"), (dsk, dsk_d, "dsk"), (sng, sng_d, "sng"), (scw, scw_d, "scw")):
                dma("sp", t[:], d[l, :, :], [], [K(nm)])
            ACT(lambda e: e.activation(out=aneg[:], in_=aneg[:], func=AF.Exp), [K("aneg")], [K("aneg")])
            DVE(lambda e: e.tensor_scalar(out=aneg[:], in0=aneg[:], scalar1=-1.0, scalar2=None, op0=ALU.mult),
                [K("aneg")], [K("aneg")])
            sq = math.sqrt(D)
            for si in range(3):
                DVE(lambda e, si=si: e.scalar_tensor_tensor(out=coefA[:, si * nD:(si + 1) * nD],
                                                            in0=modt[:, par, (3 * si + 1) * nD:(3 * si + 2) * nD], scalar=1.0,
                                                            in1=ngt[:, (2 * si) * nD:(2 * si + 1) * nD], op0=ALU.add, op1=ALU.mult),
                    MODK(par, (3 * si + 1) * nD, nD) + [K("ngt"), K("coefA")], [K("coefA")])
                w = 1.0 if si == 1 else 0.5
                DVE(lambda e, si=si, w=w: e.scalar_tensor_tensor(out=coefR[:, si * nD:(si + 1) * nD],
                                                                 in0=modt[:, par, (3 * si + 2) * nD:(3 * si + 3) * nD], scalar=w * sq,
                                                                 in1=ngt[:, (2 * si + 1) * nD:(2 * si + 2) * nD], op0=ALU.mult, op1=ALU.mult),
                    MODK(par, (3 * si + 2) * nD, nD) + [K("ngt"), K("coefR")], [K("coefR")])
            DVE(lambda e: e.tensor_scalar(out=coefA[:], in0=coefA[:], scalar1=sq, scalar2=None, op0=ALU.mult),
                [K("coefA")], [K("coefA")])

        def mixer(l):
            si = 1
            prenorm(si)
            H, P_, G, N = c.H, c.P, c.G, c.N
            HB = c.HB; nblk = H // HB; bch = HB * P_ // 128
            al = Al(0)
            Yssd = [al.get([128, T], BF16) for _ in range(c.nHP)]
            Mg = [al.get([128, T], BF16) for _ in range(nD)]
            mark = al.off
            al2 = Al(c.nHP)
            dt_t, dt_k = al2.get([128, NTC, 2 * H], F32)
            adt_t, adt_k = al2.get([128, NTC, 2 * H], F32)
            ncs_t, ncs_k = al2.get([128, NTC, 2 * H], F32)
            dd_t, dd_k = al2.get([128, NTC, 2 * H], F32)
            et_t, et_k = al2.get([128, NTC, 2 * H], F32)
            wdt_t, wdt_k = al2.get([128, nD, 2 * H], BF16)
            BT, BTk = al2.get([128, T], BF16)
            CT, CTk = al2.get([128, T], BF16)
            Btm, Btmk = al2.get([128, NTC, 128], BF16)
            xsf = [al2.get([128, T], BF16) for _ in range(bch)]
            Yg = [al2.get([128, T], F32) for _ in range(bch)]
            pp_off = al2.off
            Pp, Ppk = al2.get([128, c.NSEG, 259], F32)
            cacc, cacck = al2.get([128, T], F32)
            pp_end = al2.off
            RD, RDk = al2.get([128, HB, 128], F32)
            Ec, Eck = al2.get([128, HB, 128], F32)
            Mt, Mtk = al2.get([128, HB, 128], BF16)
            Cp, Cpk = al2.get([128, HB, 128], BF16)
            Gm, Gmk = al2.get([128, 128], F32)
            xd, xdk = al2.get([128, HB * P_], BF16)
            xdd, xddk = al2.get([128, HB * P_], BF16)
            need_b = 2 * 2048 * HB // 4 + 2 * 1024 * HB // 4 + 1024 + 2 * max(1024, HB * P_ * 2)
            if pp_end - pp_off >= need_b:
                alB = Al(0); alB.off = pp_off
            else:
                alB = al2; pp_end = NSLOT * SB
            tB = []
            for shp, dty in (([128, HB, 128], F32), ([128, HB, 128], F32), ([128, HB, 128], BF16), ([128, HB, 128], BF16), ([128, 128], F32),
                             ([128, HB * P_], BF16), ([128, HB * P_], BF16)):
                v_, k_ = alB.get(shp, dty)
                tB += [v_, k_]
            assert alB.off <= pp_end, "chain-B temporaries do not fit in conv scratch"
            chains = [{"t": [RD, RDk, Ec, Eck, Mt, Mtk, Cp, Cpk, Gm, Gmk, xd, xdk, xdd, xddk], "banks": (0, 1, 2)},
                      {"t": tB, "banks": (3, 4, 5)}]
            hst = [al2.get([128, HB * P_], F32) for _ in range(2)]
            hbf = [al2.get([128, HB * P_], BF16) for _ in range(2)]
            h0s, h0sk = al2.get([128, N], F32)
            hot, hotk = al2.get([128, bch, N], F32)

            for t0 in range(0, nD, 8):
                pass
            dma("pool", wdt_t[:, :, :], w_in[l, :, c.o_dt:c.o_dt + 2 * H].rearrange("(kc p) n -> p kc n", p=128), [], wdt_k)
            b = 4
            for tc in range(NTC):
                for kc in range(nD):
                    PE(lambda e, tc=tc, kc=kc: e.matmul(PS[b][:, tc * 2 * H:(tc + 1) * 2 * H], lhsT=U[:, kc, tc * 128:(tc + 1) * 128],
                                                        rhs=wdt_t[:, kc, :], start=(kc == 0), stop=(kc == nD - 1)),
                       wdt_k + UK(kc, tc // 4), [K("ps", b)])
            psv = PS[b][:, 0:NTC * 2 * H].rearrange("p (a b) -> p a b", a=NTC)
            DVE(lambda e: e.tensor_tensor(out=dt_t, in0=psv, in1=dtb[:].unsqueeze(1).broadcast_to([128, NTC, 2 * H]), op=ALU.add),
                [K("ps", b), K("dtb")], dt_k)
            ACT(lambda e: e.activation(out=dt_t, in_=dt_t, func=AF.Exp), dt_k, dt_k)
            ACT(lambda e: e.activation(out=dt_t, in_=dt_t, func=AF.Ln, bias=1.0), dt_k, dt_k)
            DVE(lambda e: e.tensor_tensor(out=adt_t, in0=dt_t, in1=aneg[:].unsqueeze(1).broadcast_to([128, NTC, 2 * H]), op=ALU.mult),
                dt_k + [K("aneg")], adt_k)
            W2 = NTC * 2 * H
            adt_flat = adt_t.rearrange("p a b -> p (a b)")
            b0, b1 = 5, 6
            PE(lambda e: e.matmul(PS[b0][:, 0:W2], lhsT=ones_f[:], rhs=adt_flat, start=True, stop=True),
               [K("ones_f")] + adt_k, [K("ps", b0)])
            for tc in range(NTC):
                PE(lambda e, tc=tc: e.matmul(PS[b1][:, tc * 2 * H:tc * 2 * H + H], lhsT=mk_f[:], rhs=adt_t[:, tc, 0:H], start=True, stop=True),
                   [K("mk_f")] + adt_k, [K("ps", b1)])
                PE(lambda e, tc=tc: e.matmul(PS[b1][:, tc * 2 * H + H:(tc + 1) * 2 * H], lhsT=mk_b[:], rhs=adt_t[:, tc, H:2 * H], start=True, stop=True),
                   [K("mk_b")] + adt_k, [K("ps", b1)])
            tot_v = PS[b0][:, 0:W2].rearrange("p (a b) -> p a b", a=NTC)
            cs_v = PS[b1][:, 0:W2].rearrange("p (a b) -> p a b", a=NTC)
            DVE(lambda e: e.tensor_scalar(out=ncs_t, in0=cs_v, scalar1=-1.0, scalar2=None, op0=ALU.mult), [K("ps", b1)], ncs_k)
            DVE(lambda e: e.tensor_tensor(out=dd_t, in0=tot_v, in1=ncs_t, op=ALU.add), [K("ps", b0)] + ncs_k, dd_k)
            ACT(lambda e: e.activation(out=dd_t, in_=dd_t, func=AF.Exp), dd_k, dd_k)
            DVE(lambda e: e.tensor_tensor(out=dd_t, in0=dd_t, in1=dt_t, op=ALU.mult), dd_k + dt_k, dd_k)
            ACT(lambda e: e.activation(out=et_t, in_=tot_v, func=AF.Exp), [K("ps", b0)], et_k)

            spt = 512 // c.SEG

            def conv_from_pp(wt, wkey, wbase, ntap, right):
                NS = c.NSEG
                if NS > 1:
                    DVE(lambda e: e.tensor_scalar(out=Pp[:, 1:NS, 0:1], in0=Pp[:, 0:NS - 1, 256:257], scalar1=keep[:, 0:1], scalar2=None, op0=ALU.mult),
                        Ppk + [K("keep")], Ppk)
                    DVE(lambda e: e.tensor_scalar(out=Pp[:, 0:NS - 1, 257:257 + right], in0=Pp[:, 1:NS, 1:1 + right], scalar1=keep[:, 0:1], scalar2=None, op0=ALU.mult),
                        Ppk + [K("keep")], Ppk)
                cv3 = cacc.rearrange("p (a b) -> p a b", a=NS)
                DVE(lambda e: e.tensor_scalar(out=cv3, in0=Pp[:, :, 0:256], scalar1=wt[:, wbase:wbase + 1], scalar2=None, op0=ALU.mult),
                    Ppk + [K(wkey)], cacck)
                for k in range(1, ntap):
                    DVE(lambda e, k=k: e.scalar_tensor_tensor(out=cv3, in0=Pp[:, :, k:k + 256], scalar=wt[:, wbase + k:wbase + k + 1],
                                                              in1=cv3, op0=ALU.mult, op1=ALU.add),
                        Ppk + cacck + [K(wkey)], cacck)

            DVE(lambda e: e.memset(Pp, 0.0), [], Ppk)

            def xbc_chunk(ch, dst, dstk):
                def evac(ci, tt, ps, pk):
                    ACT(lambda e: e.activation(out=Pp[:, tt * spt:(tt + 1) * spt, 1:257], in_=ps.rearrange("p (a b) -> p a b", a=spt), func=AF.Copy),
                        [pk], Ppk)
                DVE(lambda e: e.memset(Pp[:, 0, 0:1], 0.0), [], Ppk)
                DVE(lambda e: e.memset(Pp[:, c.NSEG - 1, 257:259], 0.0), [], Ppk)
                gemm(w_in, l, D, c.o_xbc + ch * 128, 128, rhsU, evac)
                conv_from_pp(convw, "convw", ch * 4, 4, 2)
                ACT(lambda e: e.activation(out=dst, in_=cacc, func=AF.Silu, bias=convb[:, ch:ch + 1]), cacck + [K("convb")], dstk)

            def ssd_unit(ch, l, blk, hb0, d, ui, tc, nord, ywritten):
                RD, RDk, Ec, Eck, Mt, Mtk, Cp, Cpk, Gm, Gmk, xd, xdk, xdd, xddk = ch["t"]
                bA, bB, bC = ch["banks"]
                mk = mk_f if d == 0 else mk_b
                mkn = "mk_f" if d == 0 else "mk_b"
                hs, hsk = hst[d]; hb_, hbk = hbf[d]
                seg = tc // 2
                seg_first = (tc % 2 == 0) if d == 0 else (tc % 2 == 1)
                seg_last = not seg_first
                tsl = slice(tc * 128, (tc + 1) * 128)
                hsl = slice(d * H + hb0, d * H + hb0 + HB)
                W_ = HB * P_
                if ui == 0:
                    for j in range(bch):
                        r0 = (blk * bch + j) * 128
                        dma("sp", h0s, h0[l, d, r0:r0 + 128, :], [], h0sk)
                        PE(lambda e, j=j: e.matmul(PS[bA][:, 128 + j * 128:128 + (j + 1) * 128], lhsT=h0s, rhs=ident_f[:], start=True, stop=True),
                           h0sk + [K("ident_f")], [K("ps", bA)])
                    ACT(lambda e: e.activation(out=hs, in_=PS[bA][:, 128:128 + W_], func=AF.Copy), [K("ps", bA)], hsk)
                    ACT(lambda e: e.activation(out=hb_, in_=PS[bA][:, 128:128 + W_], func=AF.Copy), [K("ps", bA)], hbk)
                    yield
                elif seg_first:
                    DVE(lambda e: e.tensor_scalar(out=hs, in0=hs, scalar1=keep[:, 0:1], scalar2=None, op0=ALU.mult), hsk + [K("keep")], hsk)
                    ACT(lambda e: e.activation(out=hb_, in_=hs, func=AF.Copy), hsk, hbk)
                    yield
                PE(lambda e: e.matmul(PS[bA][:, 0:128], lhsT=BT[:, tsl], rhs=CT[:, tsl], start=True, stop=True), BTk + CTk, [K("ps", bA)])
                DVE(lambda e: e.tensor_tensor(out=RD, in0=mk[:].unsqueeze(1).broadcast_to([128, HB, 128]),
                                              in1=adt_t[:, tc, hsl].unsqueeze(2).broadcast_to([128, HB, 128]), op=ALU.mult),
                    [K(mkn)] + adt_k, RDk)
                yield
                PE(lambda e: e.matmul(PS[bB][:, 0:HB * 128], lhsT=ones_f[:], rhs=RD.rearrange("p a b -> p (a b)"), start=True, stop=True),
                   [K("ones_f")] + RDk, [K("ps", bB)])
                DVE(lambda e: e.tensor_tensor(out=Gm, in0=PS[bA][:, 0:128], in1=mk[:], op=ALU.mult), [K("ps", bA), K(mkn)], Gmk)
                yield
                csb = PS[bB][:, 0:HB * 128].rearrange("p (a b) -> p a b", a=HB)
                for j in range(bch):
                    PE(lambda e, j=j: e.matmul(PS[bA][:, 128 + j * 128:128 + (j + 1) * 128], lhsT=xsf[j][0][:, tsl], rhs=ident_b[:], start=True, stop=True),
                       xsf[j][1] + [K("ident_b")], [K("ps", bA)])
                DVE(lambda e: e.tensor_tensor(out=RD, in0=csb, in1=ncs_t[:, tc, hsl].unsqueeze(2).broadcast_to([128, HB, 128]), op=ALU.add),
                    [K("ps", bB)] + ncs_k, RDk)
                yield
                ACT(lambda e: e.activation(out=Ec, in_=csb, func=AF.Exp), [K("ps", bB)], Eck)
                DVE(lambda e: e.tensor_scalar(out=RD, in0=RD, scalar1=0.0, scalar2=None, op0=ALU.min), RDk, RDk)
                yield
                ACT(lambda e: e.activation(out=RD, in_=RD, func=AF.Exp), RDk, RDk)
                xps = PS[bA][:, 128:128 + W_].rearrange("p (a b) -> p a b", a=HB)
                DVE(lambda e: e.tensor_tensor(out=xd.rearrange("p (a b) -> p a b", a=HB), in0=xps,
                                              in1=dt_t[:, tc, hsl].unsqueeze(2).broadcast_to([128, HB, P_]), op=ALU.mult),
                    [K("ps", bA)] + dt_k, xdk)
                yield
                DVE(lambda e: e.tensor_tensor(out=xdd.rearrange("p (a b) -> p a b", a=HB), in0=xps,
                                              in1=dd_t[:, tc, hsl].unsqueeze(2).broadcast_to([128, HB, P_]), op=ALU.mult),
                    [K("ps", bA)] + dd_k, xddk)
                yield
                DVE(lambda e: e.tensor_tensor(out=Cp, in0=Ec, in1=CT[:, tsl].unsqueeze(1).broadcast_to([128, HB, 128]), op=ALU.mult),
                    Eck + CTk, Cpk)
                yield
                DVE(lambda e: e.scalar_tensor_tensor(out=Mt, in0=RD, scalar=1.0, in1=Gm.unsqueeze(1).broadcast_to([128, HB, 128]),
                                                     op0=ALU.min, op1=ALU.mult), RDk + Gmk, Mtk)
                yield
                for hh in range(HB):
                    po = (hh % 2) * 64; co = (hh // 2) * 128
                    PE(lambda e, hh=hh, po=po, co=co: e.matmul(PS[bC][po:po + 64, co:co + 128], lhsT=xd[:, hh * P_:(hh + 1) * P_],
                                                               rhs=Mt[:, hh, :], start=True, stop=False),
                       xdk + Mtk, [K("ps", bC)])
                    PE(lambda e, hh=hh, po=po, co=co: e.matmul(PS[bC][po:po + 64, co:co + 128], lhsT=hb_[:, hh * P_:(hh + 1) * P_],
                                                               rhs=Cp[:, hh, :], start=False, stop=True),
                       hbk + Cpk, [K("ps", bC)])
                PE(lambda e: e.matmul(PS[bC][:, 256:256 + W_], lhsT=Btm[:, tc, :], rhs=xdd, start=True, stop=True), Btmk + xddk, [K("ps", bC)])
                yield
                for j in range(bch):
                    yv, yk = Yg[j]
                    ykk = [yk[(tc * 512) // 1024]]
                    if (j, tc) not in ywritten:
                        ywritten.add((j, tc))
                        ACT(lambda e, j=j, yv=yv: e.activation(out=yv[:, tsl], in_=PS[bC][:, j * 128:(j + 1) * 128], func=AF.Copy),
                            [K("ps", bC)], ykk)
                    else:
                        DVE(lambda e, j=j, yv=yv: e.tensor_tensor(out=yv[:, tsl], in0=PS[bC][:, j * 128:(j + 1) * 128], in1=yv[:, tsl], op=ALU.add),
                            [K("ps", bC)] + ykk, ykk)
                yield
                DVE(lambda e: e.tensor_tensor(out=hs.rearrange("p (a b) -> p a b", a=HB), in0=hs.rearrange("p (a b) -> p a b", a=HB),
                                              in1=et_t[:, tc, hsl].unsqueeze(2).broadcast_to([128, HB, P_]), op=ALU.mult),
                    hsk + et_k, hsk)
                yield
                DVE(lambda e: e.tensor_tensor(out=hs, in0=PS[bC][:, 256:256 + W_], in1=hs, op=ALU.add), [K("ps", bC)] + hsk, hsk)
                yield
                if ui != nord - 1:
                    ACT(lambda e: e.activation(out=hb_, in_=hs, func=AF.Copy), hsk, hbk)
                if seg_last:
                    for j in range(bch):
                        PE(lambda e, j=j: e.matmul(PS[bA][:, 128 + j * 128:128 + (j + 1) * 128], lhsT=hs[:, j * 128:(j + 1) * 128], rhs=ident_f[:],
                                                   start=True, stop=True), hsk + [K("ident_f")], [K("ps", bA)])
                    yield
                    ACT(lambda e: e.activation(out=hot, in_=PS[bA][:, 128:128 + bch * 128].rearrange("p (a b) -> p a b", a=bch), func=AF.Copy),
                        [K("ps", bA)], hotk)
                    r0 = blk * bch * 128
                    dma("sp", hout[l, seg, d, r0:r0 + bch * 128, :].rearrange("(a p) n -> p a n", p=128), hot, hotk, [K("hout", l, seg, d, blk)])
                yield

            import os
            MIXSUB = int(os.environ.get("MIXSUB", "7"))
            first_sq = {"v": True}
            for blk in range(nblk if (MIXSUB & 1) else 0):
                g = (blk * HB) // c.hg
                hb0 = blk * HB
                if blk == 0 or g != ((blk - 1) * HB) // c.hg:
                    xbc_chunk(c.nHP + g, BT, BTk)
                    xbc_chunk(c.nHP + G + g, CT, CTk)
                    for tc in range(NTC):
                        bb = 6 + (tc % 2)
                        PE(lambda e, tc=tc, bb=bb: e.matmul(PS[bb][:, 0:128], lhsT=BT[:, tc * 128:(tc + 1) * 128], rhs=ident_b[:], start=True, stop=True),
                           BTk + [K("ident_b")], [K("ps", bb)])
                        ACT(lambda e, tc=tc, bb=bb: e.activation(out=Btm[:, tc, :], in_=PS[bb][:, 0:128], func=AF.Copy), [K("ps", bb)], Btmk)
                for j in range(bch):
                    xbc_chunk(blk * bch + j, xsf[j][0], xsf[j][1])
                ywritten = set()
                if _os.environ.get("KOLD", "0") == "1":
                    for d_ in range(2):
                        for ui in range(NTC):
                            for _ in ssd_unit(chains[1 - d_ if _os.environ.get("KSWAP", "0") == "1" else d_], l, blk, hb0, d_, ui, ui if d_ == 0 else NTC - 1 - ui, NTC, ywritten):
                                pass
                for ui in range(NTC if _os.environ.get("KOLD", "0") != "1" else 0):
                    gens = [ssd_unit(chains[0], l, blk, hb0, 0, ui, ui, NTC, ywritten),
                            ssd_unit(chains[1], l, blk, hb0, 1, ui, NTC - 1 - ui, NTC, ywritten)]
                    live = list(gens)
                    if _os.environ.get("KSEQ", "0") == "1":
                        for gobj in gens:
                            for _ in gobj:
                                pass
                        live = []
                    while live:
                        for gobj in list(live):
                            try:
                                next(gobj)
                            except StopIteration:
                                live.remove(gobj)
                    bg_tick()
                    bg_tick()
                for j in range(bch):
                    ch = blk * bch + j

                    def evac(ci, tt, ps, pk, j=j, ch=ch):
                        yv, yk = Yg[j]
                        t, tk = tmpf()
                        ACT(lambda e: e.activation(out=t, in_=ps, func=AF.Silu), [pk], tk)
                        DVE(lambda e: e.scalar_tensor_tensor(out=yv[:, ts(tt)], in0=xsf[j][0][:, ts(tt)], scalar=dsk[:, ch:ch + 1], in1=yv[:, ts(tt)],
                                                             op0=ALU.mult, op1=ALU.add), xsf[j][1] + yk + [K("dsk")], yk)
                        DVE(lambda e: e.tensor_tensor(out=yv[:, ts(tt)], in0=yv[:, ts(tt)], in1=t, op=ALU.mult), yk + tk, yk)
                        acc_add(first_sq["v"] and ch == 0, yv[:, ts(tt)], yk, tt)
                        ACT(lambda e: e.activation(out=Yssd[ch][0][:, ts(tt)], in_=yv[:, ts(tt)], func=AF.Identity, scale=sng[:, ch:ch + 1]),
                            yk + [K("sng")], [Yssd[ch][1][tt]])
                    gemm(w_in, l, D, c.o_z + ch * 128, 128, rhsU, evac)
                first_sq["v"] = False
            for tt in range(NTT):
                bq = 6 + tt
                PE(lambda e, bq=bq, tt=tt: e.matmul(PS[bq][:, :], lhsT=ones_f[:], rhs=acc[:, ts(tt)], start=True, stop=True),
                   [K("ones_f"), K("acc", tt)], [K("ps", bq)])
                DVE(lambda e, bq=bq, tt=tt: e.tensor_scalar(out=rstd[:, ts(tt)], in0=PS[bq][:, :], scalar1=float(1.0 / c.HP), scalar2=float(NORM_EPS),
                                                            op0=ALU.mult, op1=ALU.add), [K("ps", bq)], [K("acc", tt)])
                ACT(lambda e, tt=tt: e.activation(out=rstd[:, ts(tt)], in_=rstd[:, ts(tt)], func=AF.Sqrt), [K("acc", tt)], [K("acc", tt)])
                DVE(lambda e, tt=tt: e.reciprocal(out=rstd[:, ts(tt)], in_=rstd[:, ts(tt)]), [K("acc", tt)], [K("acc", tt)])

            al3 = Al(c.nHP + nD)
            SG = [al3.get([128, T], BF16) for _ in range(2)]

            mstate = {"first": True}

            def branch_merge(bi, Ysrc, Kdim, use_rstd):
                is_first = mstate["first"]; mstate["first"] = False
                for d0 in range(0, nD, 2):
                    def evac_gate(ci, tt, ps, pk):
                        sv, sk = SG[ci]
                        ACT(lambda e: e.activation(out=sv[:, ts(tt)], in_=ps, func=AF.Sigmoid), [pk], [sk[tt]])
                        if use_rstd:
                            DVE(lambda e: e.tensor_tensor(out=sv[:, ts(tt)], in0=sv[:, ts(tt)], in1=rstd[:, ts(tt)], op=ALU.mult),
                                [sk[tt], K("acc", tt)], [sk[tt]])

                    def evac_br(ci, tt, ps, pk, d0=d0):
                        sv, sk = SG[ci]
                        mv, mkk = Mg[d0 + ci]
                        if is_first:
                            DVE(lambda e: e.tensor_tensor(out=mv[:, ts(tt)], in0=ps, in1=sv[:, ts(tt)], op=ALU.mult), [pk, sk[tt]], [mkk[tt]])
                        else:
                            t, tk = tmpf()
                            DVE(lambda e: e.tensor_tensor(out=t, in0=ps, in1=sv[:, ts(tt)], op=ALU.mult), [pk, sk[tt]], tk)
                            DVE(lambda e: e.tensor_tensor(out=mv[:, ts(tt)], in0=mv[:, ts(tt)], in1=t, op=ALU.add), tk + [mkk[tt]], [mkk[tt]])
                    ncol = min(256, (nD - d0) * 128)
                    gemm(w_in, l, D, c.o_gate + bi * D + d0 * 128, ncol, rhsU, evac_gate)
                    gemm(w_br[bi], l, Kdim, d0 * 128, ncol, lambda kc: (Ysrc[kc][0], Ysrc[kc][1]), evac_br)

            if MIXSUB & 1:
                branch_merge(0, Yssd, c.HP, True)

            al4 = Al(0)
            Ysc = [al4.get([128, T], BF16) for _ in range(c.nSC)]
            al5 = Al(c.nHP + nD + 2)
            t1, t1k = al5.get([128, T], F32)
            Pp2, Pp2k = al5.get([128, c.NSEG, 259], F32)
            cacc2, cacc2k = al5.get([128, T], F32)
            DVE(lambda e: e.memset(Pp2, 0.0), [], Pp2k)
            NS = c.NSEG
            for ch in range(c.nSC if (MIXSUB & 2) else 0):
                def ev_c(ci, tt, ps, pk):
                    ACT(lambda e: e.activation(out=t1[:, ts(tt)], in_=ps, func=AF.Copy), [pk], t1k)

                def ev_x(ci, tt, ps, pk):
                    DVE(lambda e: e.tensor_tensor(out=Pp2[:, tt * spt:(tt + 1) * spt, 1:257], in0=ps.rearrange("p (a b) -> p a b", a=spt),
                                                  in1=t1[:, ts(tt)].rearrange("p (a b) -> p a b", a=spt), op=ALU.mult), [pk] + t1k, Pp2k)
                gemm(w_in, l, D, c.o_scc + ch * 128, 128, rhsU, ev_c)
                gemm(w_in, l, D, c.o_scx + ch * 128, 128, rhsU, ev_x)
                if NS > 1:
                    DVE(lambda e: e.tensor_scalar(out=Pp2[:, 1:NS, 0:1], in0=Pp2[:, 0:NS - 1, 256:257], scalar1=keep[:, 0:1], scalar2=None, op0=ALU.mult),
                        Pp2k + [K("keep")], Pp2k)
                    DVE(lambda e: e.tensor_scalar(out=Pp2[:, 0:NS - 1, 257:258], in0=Pp2[:, 1:NS, 1:2], scalar1=keep[:, 0:1], scalar2=None, op0=ALU.mult),
                        Pp2k + [K("keep")], Pp2k)
                cv3 = cacc2.rearrange("p (a b) -> p a b", a=NS)
                DVE(lambda e, ch=ch: e.tensor_scalar(out=cv3, in0=Pp2[:, :, 0:256], scalar1=scw[:, ch * 3:ch * 3 + 1], scalar2=None, op0=ALU.mult),
                    Pp2k + [K("scw")], cacc2k)
                for k in range(1, 3):
                    DVE(lambda e, k=k, ch=ch: e.scalar_tensor_tensor(out=cv3, in0=Pp2[:, :, k:k + 256], scalar=scw[:, ch * 3 + k:ch * 3 + k + 1], in1=cv3,
                                                                     op0=ALU.mult, op1=ALU.add), Pp2k + cacc2k + [K("scw")], cacc2k)

                def ev_b(ci, tt, ps, pk, ch=ch):
                    DVE(lambda e: e.tensor_tensor(out=Ysc[ch][0][:, ts(tt)], in0=ps, in1=cacc2[:, ts(tt)], op=ALU.mult), [pk] + cacc2k, [Ysc[ch][1][tt]])
                gemm(w_in, l, D, c.o_scb + ch * 128, 128, rhsU, ev_b)
            if MIXSUB & 2:
                branch_merge(1, Ysc, c.SCW, False)

            al6 = Al(0)
            Yft = [al6.get([128, T], BF16) for _ in range(c.nFT)]
            al7 = Al(c.nHP + nD + 2)
            Ftc = [(al6 if c.nFT + 2 <= c.nHP else al7).get([128, T], BF16) for _ in range(2)]
            AB, ABk = al7.get([128, NTC, 512], BF16)
            clr = [al7.get([128, T], BF16) for _ in range(2)]
            slr = [al7.get([128, T], BF16) for _ in range(2)]
            for q in range(c.FTG if (MIXSUB & 4) else 0):
                for j in range(2):
                    def ev_f(ci, tt, ps, pk, j=j):
                        ACT(lambda e: e.activation(out=Ftc[j][0][:, ts(tt)], in_=ps, func=AF.Copy), [pk], [Ftc[j][1][tt]])
                    gemm(w_in, l, D, c.o_ft + (q * 2 + j) * 128, 128, rhsU, ev_f)
                for tc in range(NTC):
                    bb = 4 + (tc % 2)
                    for j in range(2):
                        PE(lambda e, tc=tc, j=j, bb=bb: e.matmul(PS[bb][:, :], lhsT=Ftc[j][0][:, tc * 128:(tc + 1) * 128], rhs=cst[:, j, :],
                                                                 start=(j == 0), stop=(j == 1)), Ftc[j][1] + [K("cst")], [K("ps", bb)])
                    ACT(lambda e, tc=tc, bb=bb: e.activation(out=AB[:, tc, :], in_=PS[bb][:, :], func=AF.Copy), [K("ps", bb)], ABk)
                for tc in range(NTC):
                    r = tc % 2
                    dma("pool", clr[r][0], cl_d[tc * 128:(tc + 1) * 128, :], [], clr[r][1])
                    dma("pool", slr[r][0], sl_d[tc * 128:(tc + 1) * 128, :], [], slr[r][1])
                    for jj in range(2):
                        for tt in range(NTT):
                            bb = jj * NTT + tt
                            PE(lambda e, tc=tc, jj=jj, tt=tt, bb=bb, r=r: e.matmul(PS[bb][:, :], lhsT=AB[:, tc, jj * 128:(jj + 1) * 128], rhs=clr[r][0][:, ts(tt)],
                                                                                   start=(tc == 0), stop=False), ABk + clr[r][1], [K("ps", bb)])
                            PE(lambda e, tc=tc, jj=jj, tt=tt, bb=bb, r=r: e.matmul(PS[bb][:, :], lhsT=AB[:, tc, 256 + jj * 128:256 + (jj + 1) * 128], rhs=slr[r][0][:, ts(tt)],
                                                                                   start=False, stop=(tc == NTC - 1)), ABk + slr[r][1], [K("ps", bb)])
                for jj in range(2):
                    for tt in range(NTT):
                        bb = jj * NTT + tt
                        yv, yk = Yft[q * 2 + jj]
                        ACT(lambda e, bb=bb, yv=yv, tt=tt: e.activation(out=yv[:, ts(tt)], in_=PS[bb][:, :], func=AF.Copy), [K("ps", bb)], [yk[tt]])
            if MIXSUB & 4:
                branch_merge(2, Yft, c.FTW, False)

            ff = {"v": True}
            mk2 = out_evac_factory(ff)
            for d0 in range(0, nD, 2):
                gemm(w_out, l, D, d0 * 128, min(256, (nD - d0) * 128), lambda kc: (Mg[kc][0], Mg[kc][1]), mk2(d0))
                ff["v"] = False
            post_update(si)

        import os
        STAGE = int(os.environ.get("KSTAGE", "99"))
        for l in range(L if STAGE >= 2 else 0):
            layer_consts(l)
            if l + 1 < L:
                st["bg"] = ada_start(l + 1)
            if STAGE >= 3:
                ffn(l, 0, 0)
            if STAGE >= 4:
                mixer(l)
            if STAGE >= 5:
                ffn(l, 1, 2)
            if st["bg"] is not None:
                for _ in st["bg"]:
                    pass
                st["bg"] = None

        ostg, ostgk = hview(0, [128, 2, D], F32)
        for tc in range(NTC):
            r = tc % 2
            for d4 in range(0, nD, 4):
                b = (d4 // 4) % 8
                for j in range(min(4, nD - d4)):
                    dc = d4 + j
                    PE(lambda e, b=b, j=j, dc=dc, tc=tc: e.matmul(PS[b][:, j * 128:(j + 1) * 128], lhsT=X[:, dc, tc * 128:(tc + 1) * 128], rhs=ident_f[:],
                                                                  start=True, stop=True), XK(dc, tc // 4) + [K("ident_f")], [K("ps", b)])
                w = min(4, nD - d4) * 128
                ACT(lambda e, b=b, r=r, d4=d4, w=w: e.activation(out=ostg[:, r, d4 * 128:d4 * 128 + w], in_=PS[b][:, 0:w], func=AF.Copy),
                    [K("ps", b)], [K("ostg", r)] + ostgk)
            dma("sp", yout[tc * 128:(tc + 1) * 128, :], ostg[:, r, :], [K("ostg", r)] + ostgk, [K("yout", tc)])
        outk = [k for k in pg.lastw if k[0] in ("yout", "hout")]
        pg.add("sp", lambda e: None, outk, [])

        pg.emit(nc, sems, dsems, block)
    return nc


def _fm(v, n=128):
    sh = v.shape
    return np.ascontiguousarray(np.swapaxes(v.reshape(sh[:-1] + (sh[-1] // n, n)), -1, -2))


def _dft_tables(T, seqlen, ftgd):
    t = np.arange(T)
    blk = (t[:, None] // seqlen) == (t[None, :] // seqlen)
    ang = 2.0 * np.pi * ((t[:, None] % seqlen) * (t[None, :] % seqlen) % seqlen) / seqlen
    s = 1.0 / math.sqrt(seqlen)
    cl = np.where(blk, np.cos(ang) * s, 0.0).astype(np.float32)
    sl = np.where(blk, -np.sin(ang) * s, 0.0).astype(np.float32)
    k = np.arange(ftgd)
    a2 = 2.0 * np.pi * ((k[:, None] * k[None, :]) % ftgd) / ftgd
    s2 = 1.0 / math.sqrt(ftgd)
    cs = np.concatenate([np.cos(a2) * s2, np.sin(a2) * s2], axis=1).astype(np.float32)
    return cl, sl, cs.reshape(2, 128, 512)


def _pos_emb(n_tok, D, grid_w=64, base=10000.0):
    t = np.arange(n_tok)
    r = (t // grid_w).astype(np.float32)[:, None]
    col = (t % grid_w).astype(np.float32)[:, None]
    nf = D // 4
    omega = (1.0 / (np.float32(base) ** (np.arange(nf, dtype=np.float32) / np.float32(nf)))).astype(np.float32)
    return np.concatenate([np.sin(r * omega), np.cos(r * omega), np.sin(col * omega), np.cos(col * omega)], axis=-1).astype(np.float32)


def run(cfg, inp, n_prompt_cores, n_sample_cores, trace=False):
    c = cfg
    f = lambda a: np.ascontiguousarray(np.asarray(a, dtype=np.float32))
    L, D, T = c.DEPTH, c.D, c.T
    xp = f(inp["x_prompt"]); xs = f(inp["x_sample"]); st_ = f(inp["state_ssd"])
    spc = T // c.SEG
    shared = {
        "ada_w": f(inp["ada_w"]),
        "ada_b": _fm(f(inp["ada_b"]).reshape(L, 9, D)).transpose(0, 2, 1, 3).reshape(L, 128, 9 * c.nD).copy(),
        "norm_g": _fm(f(inp["norm_g"])).transpose(0, 2, 1, 3).reshape(L, 128, 6 * c.nD).copy(),
        "ffn1_wgu": f(inp["ffn1_wgu"]), "ffn2_wgu": f(inp["ffn2_wgu"]), "ffn1_wd": f(inp["ffn1_wd"]), "ffn2_wd": f(inp["ffn2_wd"]),
        "w_in": f(inp["w_in"]), "w_br_ssd": f(inp["w_br_ssd"]), "w_br_sc": f(inp["w_br_sc"]), "w_br_ft": f(inp["w_br_ft"]),
        "w_out": f(inp["w_out"]),
        "conv_w": _fm(f(inp["ssd_conv_w"])).transpose(0, 2, 3, 1).reshape(L, 128, c.nXBC * 4).copy(),
        "conv_b": _fm(f(inp["ssd_conv_b"])),
        "dtb": np.broadcast_to(f(inp["ssd_dt_bias"]).reshape(L, 1, 2 * c.H), (L, 128, 2 * c.H)).copy(),
        "alog": np.broadcast_to(f(inp["ssd_a_log"]).reshape(L, 1, 2 * c.H), (L, 128, 2 * c.H)).copy(),
        "dsk": _fm(np.repeat(f(inp["ssd_d"]), c.P, axis=-1)),
        "sng": _fm(f(inp["ssd_norm_g"])),
        "scw": _fm(f(inp["sc_conv_w"])).transpose(0, 2, 3, 1).reshape(L, 128, c.nSC * 3).copy(),
    }
    cl_p, sl_p, cs = _dft_tables(T, c.SEG, c.FTGD)
    cl_s, sl_s, _ = _dft_tables(T, T, c.FTGD)
    pe = _pos_emb(T, D)
    zeros_pos = np.zeros((T, D), np.float32)
    zeros_h0 = np.zeros((L, 2, c.HP, c.N), np.float32)
    in_maps = []
    for ci in range(n_prompt_cores):
        m = dict(shared)
        m["xin"] = xp[ci * spc:(ci + 1) * spc].reshape(T, D)
        m["pos"] = zeros_pos
        m["cv"] = _fm(f(inp["c_ctx"]))
        m["keep"] = np.zeros((128, 1), np.float32)
        m["h0"] = zeros_h0
        m["dft_cs"] = cs; m["dft_cl"] = cl_p; m["dft_sl"] = sl_p
        in_maps.append(m)
    for b in range(n_sample_cores):
        m = dict(shared)
        m["xin"] = xs[b]
        m["pos"] = pe
        m["cv"] = _fm(f(inp["c"])[b])
        m["keep"] = np.ones((128, 1), np.float32)
        m["h0"] = np.ascontiguousarray(st_[b].reshape(L, 2, c.HP, c.N))
        m["dft_cs"] = cs; m["dft_cl"] = cl_s; m["dft_sl"] = sl_s
        in_maps.append(m)
    nc = build(c)
    ncores = n_prompt_cores + n_sample_cores
    res = run_bass_kernel_spmd(nc, in_maps, core_ids=list(range(ncores)), trace=trace)
    outs = res.results
    y_prompt = np.concatenate([outs[ci]["yout"].reshape(spc, c.SEG, D) for ci in range(n_prompt_cores)], axis=0)
    y_sample = np.stack([outs[n_prompt_cores + b]["yout"] for b in range(n_sample_cores)], axis=0)
    ns = np.concatenate([np.transpose(outs[ci]["hout"], (1, 0, 2, 3, 4)) for ci in range(n_prompt_cores)], axis=0)
    new_state = ns.reshape(n_prompt_cores * spc, L, 2, c.H, c.P, c.N)
    return (y_prompt.astype(np.float32), y_sample.astype(np.float32), np.ascontiguousarray(new_state.astype(np.float32))), res


def kernel(**inputs):
    cfg = Cfg()
    outs, _ = run(cfg, inputs, 4, 2)
    return outs
```

```python
import math
import numpy as np
import concourse.bass as bass
import concourse.mybir as mybir
from concourse.bass_utils import run_bass_kernel_spmd

F32 = mybir.dt.float32
BF16 = mybir.dt.bfloat16
ALU = mybir.AluOpType
AF = mybir.ActivationFunctionType

import os as _os
SAME_ENGINE_SYNC = _os.environ.get("KSYNC", "0") == "1"
NORM_EPS = 1e-6


class Cfg:
    def __init__(self, **kw):
        self.D = 2048; self.T = 1024; self.SEG = 256; self.DEPTH = 4
        self.H = 32; self.P = 64; self.G = 4; self.N = 128; self.Q = 128
        self.SCW = 1024; self.FTW = 1024; self.FTG = 4; self.F = 5504
        self.HB = 4
        for k, v in kw.items():
            setattr(self, k, v)
        c = self
        c.nD = c.D // 128; c.HP = c.H * c.P; c.nHP = c.HP // 128
        c.XBC = c.HP + 2 * c.G * c.N; c.nXBC = c.XBC // 128
        c.nSC = c.SCW // 128; c.nFT = c.FTW // 128; c.FTGD = c.FTW // c.FTG
        c.nF = c.F // 128; c.NTC = c.T // 128; c.NTT = c.T // 512; c.NSEG = c.T // c.SEG
        c.INC = c.HP + c.XBC + 2 * c.H + 3 * c.SCW + c.FTW + 3 * c.D
        c.o_z = 0; c.o_xbc = c.HP; c.o_dt = c.HP + c.XBC; c.o_scb = c.o_dt + 2 * c.H
        c.o_scc = c.o_scb + c.SCW; c.o_scx = c.o_scc + c.SCW; c.o_ft = c.o_scx + c.SCW
        c.o_gate = c.o_ft + c.FTW
        c.hg = c.H // c.G
        assert c.FTGD == 256 and c.P == 64 and c.N == 128 and c.SEG == 256


class Op:
    __slots__ = ("eng", "fn", "deps", "dma", "dsem", "dval", "sig", "need_sig")

    def __init__(self, eng, fn, deps, dma):
        self.eng = eng; self.fn = fn; self.deps = deps; self.dma = dma
        self.dsem = None; self.dval = 0; self.sig = 0; self.need_sig = False


class Prog:
    NSEM = {"sp": 24, "pool": 8}

    def __init__(self):
        self.ops = []
        self.lastw = {}
        self.readers = {}
        self.dsem_rr = {"sp": 0, "pool": 0}
        self.dsem_cnt = {}

    def add(self, eng, fn, R=(), W=(), dma=False):
        idx = len(self.ops)
        deps = set()
        R = list(R); W = list(W)
        W += [k for k in R if k[0] == "ps"]
        R = [k for k in R if k[0] != "ps"]
        if dma:
            j = self.dsem_rr[eng]
            self.dsem_rr[eng] = (j + 1) % self.NSEM[eng]
            skey = ("dsem", eng, j)
            W.append(skey)
        for k in R:
            if k in self.lastw:
                deps.add(self.lastw[k])
        for k in W:
            if k in self.lastw:
                deps.add(self.lastw[k])
            deps |= set(self.readers.get(k, {}).values())
        for k in W:
            self.lastw[k] = idx
            self.readers[k] = {}
        for k in R:
            if k not in W:
                rk = (eng, idx) if dma else eng
                self.readers.setdefault(k, {})[rk] = idx
        deps.discard(idx)
        op = Op(eng, fn, deps, dma)
        if dma:
            c = self.dsem_cnt.get((eng, j), 0) + 1
            self.dsem_cnt[(eng, j)] = c
            op.dsem = (eng, j); op.dval = 16 * c
        self.ops.append(op)
        return idx

    def emit(self, nc, sems, dsems, block):
        ops = self.ops
        for o in ops:
            for d in o.deps:
                p = ops[d]
                if p.dma:
                    continue
                if p.eng == o.eng and (p.eng == "pe" or (not SAME_ENGINE_SYNC and p.eng in ("act", "dve"))):
                    continue
                p.need_sig = True
        cnt = {}
        for o in ops:
            if o.need_sig and not o.dma:
                cnt[o.eng] = cnt.get(o.eng, 0) + 1
                o.sig = cnt[o.eng]
        waited = {}
        plans = {e: [] for e in ("pe", "act", "dve", "pool", "sp")}
        for o in ops:
            need = {}
            for d in o.deps:
                p = ops[d]
                if p.dma:
                    key = ("d",) + p.dsem; val = p.dval
                else:
                    if p.eng == o.eng and (p.eng == "pe" or (not SAME_ENGINE_SYNC and p.eng in ("act", "dve"))):
                        continue
                    key = ("e", p.eng); val = p.sig
                if val > need.get(key, 0):
                    need[key] = val
            wl = []
            for key, val in need.items():
                if waited.get((o.eng, key), 0) >= val:
                    continue
                waited[(o.eng, key)] = val
                wl.append((key, val))
            plans[o.eng].append((o, wl))

        semv = {}
        heads = {e: 0 for e in plans}
        total = sum(len(v) for v in plans.values())
        done = 0
        while done < total:
            prog_made = False
            for en, pl in plans.items():
                while heads[en] < len(pl):
                    o, wl = pl[heads[en]]
                    if all(semv.get(key, 0) >= val for key, val in wl):
                        if o.dma:
                            semv[("d",) + o.dsem] = semv.get(("d",) + o.dsem, 0) + 16
                        elif o.need_sig:
                            semv[("e", o.eng)] = semv.get(("e", o.eng), 0) + 1
                        heads[en] += 1; done += 1; prog_made = True
                    else:
                        break
            if not prog_made:
                raise RuntimeError("DEADLOCK in sync plan: " + str({en: (heads[en], plans[en][heads[en]][1] if heads[en] < len(plans[en]) else None) for en in plans}))
        self.stats = {en: len(pl) for en, pl in plans.items()}
        self.nwaits = {en: sum(len(wl) for _, wl in pl) for en, pl in plans.items()}
        print("PROG ops per engine", self.stats, "waits", self.nwaits, "final sems", {k: v for k, v in semv.items() if k[0] == "e"})

        def run(engname, e):
            for o, wl in plans[engname]:
                for key, val in wl:
                    sem = dsems[key[1:]] if key[0] == "d" else sems[key[1]]
                    e.wait_ge(sem, val)
                ins = o.fn(e)
                if o.dma:
                    ins.then_inc(dsems[o.dsem], 16)
                elif o.need_sig:
                    ins.then_inc(sems[o.eng], 1)

        @block.tensor
        def _(e):
            run("pe", e)

        @block.scalar
        def _(e):
            run("act", e)

        @block.vector
        def _(e):
            run("dve", e)

        @block.gpsimd
        def _(e):
            run("pool", e)

        @block.sync
        def _(e):
            run("sp", e)


def build(cfg):
    c = cfg
    D, T, L = c.D, c.T, c.DEPTH
    nD, NTT, NTC = c.nD, c.NTT, c.NTC
    nc = bass.Bass("TRN2", target_bir_lowering=False)

    def din(name, shape):
        return nc.dram_tensor(name, list(shape), F32, kind="ExternalInput").ap()

    xin = din("xin", [T, D]); pos = din("pos", [T, D]); cv = din("cv", [128, nD])
    keep_d = din("keep", [128, 1]); h0 = din("h0", [L, 2, c.HP, c.N])
    ada_w = din("ada_w", [L, D, 9 * D]); ada_b = din("ada_b", [L, 128, 9 * nD])
    ng_d = din("norm_g", [L, 128, 6 * nD])
    wgu = [din("ffn1_wgu", [L, D, 2 * c.F]), din("ffn2_wgu", [L, D, 2 * c.F])]
    wd = [din("ffn1_wd", [L, c.F, D]), din("ffn2_wd", [L, c.F, D])]
    w_in = din("w_in", [L, D, c.INC])
    w_br = [din("w_br_ssd", [L, c.HP, D]), din("w_br_sc", [L, c.SCW, D]), din("w_br_ft", [L, c.FTW, D])]
    w_out = din("w_out", [L, D, D])
    convw_d = din("conv_w", [L, 128, c.nXBC * 4]); convb_d = din("conv_b", [L, 128, c.nXBC])
    dtb_d = din("dtb", [L, 128, 2 * c.H]); alog_d = din("alog", [L, 128, 2 * c.H])
    dsk_d = din("dsk", [L, 128, c.nHP]); sng_d = din("sng", [L, 128, c.nHP])
    scw_d = din("scw", [L, 128, c.nSC * 3])
    cs_d = din("dft_cs", [2, 128, 512]); cl_d = din("dft_cl", [T, T]); sl_d = din("dft_sl", [T, T])
    yout = nc.dram_tensor("yout", [T, D], F32, kind="ExternalOutput").ap()
    hout = nc.dram_tensor("hout", [L, c.NSEG, 2, c.HP, c.N], F32, kind="ExternalOutput").ap()

    pg = Prog()
    NSLOT = max(c.nF, 43) if (T >= 1024 and c.D >= 2048) else 80
    SB = T * 2
    import contextlib
    es = contextlib.ExitStack()
    with es:
        def sb(name, shape, dt):
            return es.enter_context(nc.sbuf_tensor("sb_" + name, list(shape), dt))

        X = sb("X", [128, nD, T], F32)
        U = sb("U", [128, nD, T], BF16)
        HA = sb("HA", [128, NSLOT * T], BF16)
        WR = sb("WR", [128, 4, 4 * 256], BF16)
        ident_f = sb("ident_f", [128, 128], F32)
        ident_b = sb("ident_b", [128, 128], BF16)
        ones_f = sb("ones_f", [128, 128], F32)
        mk_f = sb("mk_f", [128, 128], F32)
        mk_b = sb("mk_b", [128, 128], F32)
        modt = sb("modt", [128, 2, 9 * nD], F32)
        ngt = sb("ngt", [128, 6 * nD], F32)
        coefA = sb("coefA", [128, 3 * nD], F32)
        coefR = sb("coefR", [128, 3 * nD], F32)
        convw = sb("convw", [128, c.nXBC * 4], F32)
        convb = sb("convb", [128, c.nXBC], F32)
        dtb = sb("dtb", [128, 2 * c.H], F32)
        aneg = sb("aneg", [128, 2 * c.H], F32)
        dsk = sb("dsk", [128, c.nHP], F32)
        sng = sb("sng", [128, c.nHP], F32)
        scw = sb("scw", [128, c.nSC * 3], F32)
        keep = sb("keep", [128, 1], F32)
        scv = sb("scv", [128, nD], BF16)
        cvt = sb("cvt", [128, nD], F32)
        cst = sb("cst", [128, 2, 512], BF16)
        acc = sb("acc", [128, T], F32)
        rstd = acc
        tmpA = sb("tmpA", [128, 2, 512], F32)
        PS = [es.enter_context(nc.psum_tensor("ps%d" % b, [128, 512], F32)) for b in range(8)]
        sems = {e: es.enter_context(nc.semaphore("s_" + e)) for e in ("pe", "act", "dve", "pool", "sp")}
        dsems = {}
        for q, n in Prog.NSEM.items():
            for j in range(n):
                dsems[(q, j)] = es.enter_context(nc.semaphore("d_%s%d" % (q, j)))
        block = es.enter_context(nc.Block())

        def hview(off_b, shape, dt):
            esz = 4 if dt == F32 else 2
            n = int(np.prod(shape[1:]))
            e0 = off_b // 2
            v = HA[:, e0:e0 + n * esz // 2]
            if dt == F32:
                v = v.bitcast(F32)
            if len(shape) == 3:
                v = v.rearrange("p (a b) -> p a b", a=shape[1])
            elif len(shape) == 4:
                v = v.rearrange("p (a b c) -> p a b c", a=shape[1], b=shape[2])
            keys = [("H", g) for g in range(off_b // 1024, (off_b + n * esz + 1023) // 1024)]
            return v, keys

        class Al:
            def __init__(self, start_slot):
                self.off = start_slot * SB

            def get(self, shape, dt):
                esz = 4 if dt == F32 else 2
                n = int(np.prod(shape[1:])) * esz
                n = (n + 1023) // 1024 * 1024
                v, k = hview(self.off, shape, dt)
                self.off += n
                assert self.off <= NSLOT * SB, "HA arena overflow"
                return v, k

        def K(name, *idx):
            return (name,) + idx

        def XK(dc, tt=None):
            return [("X", dc, t) for t in (range(NTT) if tt is None else [tt])]

        def UK(dc, tt=None):
            return [("U", dc, t) for t in (range(NTT) if tt is None else [tt])]

        def ts(tt):
            return slice(tt * 512, (tt + 1) * 512)

        tmp_rr = [0]

        def tmpf():
            i = tmp_rr[0]; tmp_rr[0] ^= 1
            return tmpA[:, i, :], [("tmpA", i)]

        ACT = lambda fn, R, W: pg.add("act", fn, R, W)
        DVE = lambda fn, R, W: pg.add("dve", fn, R, W)
        PE = lambda fn, R, W: pg.add("pe", fn, R, W)

        def dma(q, out, in_, R, W):
            eng = q
            return pg.add(eng, lambda e: e.dma_start(out=out, in_=in_), R, W, dma=True)

        pg.add("pool", lambda e: e.memset(ident_f[:], 0.0), [], [K("ident_f")])
        pg.add("pool", lambda e: e.affine_select(out=ident_f[:], in_=ident_f[:], pattern=[[-1, 128]],
                                                 compare_op=ALU.not_equal, fill=1.0, base=0, channel_multiplier=1),
               [], [K("ident_f")])
        pg.add("pool", lambda e: e.memset(ones_f[:], 1.0), [], [K("ones_f")])
        pg.add("pool", lambda e: e.affine_select(out=mk_f[:], in_=ones_f[:], pattern=[[1, 128]],
                                                 compare_op=ALU.is_ge, fill=0.0, base=0, channel_multiplier=-1),
               [K("ones_f")], [K("mk_f")])
        pg.add("pool", lambda e: e.affine_select(out=mk_b[:], in_=ones_f[:], pattern=[[-1, 128]],
                                                 compare_op=ALU.is_ge, fill=0.0, base=0, channel_multiplier=1),
               [K("ones_f")], [K("mk_b")])
        DVE(lambda e: e.tensor_copy(out=ident_b[:], in_=ident_f[:]), [K("ident_f")], [K("ident_b")])
        dma("sp", keep[:], keep_d[:, :], [], [K("keep")])
        dma("sp", cvt[:], cv[:, :], [], [K("cvt")])
        ACT(lambda e: e.activation(out=scv[:], in_=cvt[:], func=AF.Silu), [K("cvt")], [K("scv")])
        dma("pool", cst[:], cs_d.rearrange("j p n -> p j n"), [], [K("cst")])

        st = {"half": 0, "wslot": 0, "bg": None}

        def gemm(Wd, l, K_, col0, ncols, rhs, evac, ntok=T, tick=False, fixed_base=None):
            nkc = K_ // 128
            nch = (ncols + 127) // 128
            ntt = max(1, ntok // 512)
            nw = min(512, ntok)
            if fixed_base is None:
                base = st["half"] * 4; st["half"] ^= 1
            else:
                base = fixed_base
            for t0 in range(0, nkc, 4):
                kn = min(4, nkc - t0)
                slot = st["wslot"]; st["wslot"] = (slot + 1) % 4
                wv = WR[:, slot, :].rearrange("p (a b) -> p a b", a=4)
                src = Wd[l, t0 * 128:(t0 + kn) * 128, col0:col0 + ncols].rearrange("(kc p) n -> p kc n", p=128)
                dma("pool", wv[:, 0:kn, 0:ncols], src, [], [K("WR", slot)])
                for ci in range(nch):
                    cw = min(128, ncols - ci * 128)
                    for kk in range(kn):
                        kc = t0 + kk
                        rap, rkeys = rhs(kc)
                        for tt in range(ntt):
                            b = base + ci * ntt + tt
                            PE(lambda e, b=b, cw=cw, wv=wv, kk=kk, ci=ci, rap=rap, tt=tt, kc=kc:
                               e.matmul(PS[b][0:cw, 0:nw], lhsT=wv[:, kk, ci * 128:ci * 128 + cw],
                                        rhs=rap[:, tt * 512:tt * 512 + nw], start=(kc == 0), stop=(kc == nkc - 1)),
                               [K("WR", slot), rkeys[tt]], [K("ps", b)])
            for ci in range(nch):
                cw = min(128, ncols - ci * 128)
                for tt in range(ntt):
                    b = base + ci * ntt + tt
                    evac(ci, tt, PS[b][0:cw, 0:nw], K("ps", b))
            if tick and st["bg"] is not None:
                try:
                    next(st["bg"])
                except StopIteration:
                    st["bg"] = None

        def bg_tick():
            if st["bg"] is not None:
                try:
                    next(st["bg"])
                except StopIteration:
                    st["bg"] = None

        def rhsU(kc):
            return U[:, kc, :], [("U", kc, t) for t in range(NTT)]

        def ada_gen(l):
            par = l % 2
            for j0 in range(0, 9 * nD, 2):
                def evac(ci, tt, ps, pk, j0=j0):
                    j = j0 + ci
                    DVE(lambda e: e.tensor_tensor(out=modt[:, par, j:j + 1], in0=ps[:, 0:1], in1=adab_t[:, par, j:j + 1],
                                                  op=ALU.add), [pk, K("adab", par)], [K("modt", par, j)])
                gemm(ada_w, l, D, j0 * 128, 256, lambda kc: (scv[:, kc:kc + 1], [K("scv")]), evac, ntok=1, tick=False, fixed_base=6)
                yield

        adab_t = sb("adab_t", [128, 2, 9 * nD], F32)

        def ada_start(l):
            par = l % 2
            dma("sp", adab_t[:, par, :], ada_b[l, :, :], [], [K("adab", par)])
            return ada_gen(l)

        def MODK(par, j0, n):
            return [K("modt", par, j) for j in range(j0, j0 + n)]

        stg, stgk = hview(0, [128, 4, D], F32)
        stg2, stg2k = hview(4 * D * 4, [128, 4, D], F32)
        for tt in range(NTT):
            for j in range(4):
                tc = tt * 4 + j
                dma("sp", stg[:, j, :], xin[tc * 128:(tc + 1) * 128, :], [], [K("stg", j)] + stgk)
                dma("sp", stg2[:, j, :], pos[tc * 128:(tc + 1) * 128, :], [], [K("stg2", j)] + stg2k)
                DVE(lambda e, j=j: e.tensor_tensor(out=stg[:, j, :], in0=stg[:, j, :], in1=stg2[:, j, :], op=ALU.add),
                    [K("stg", j), K("stg2", j)] + stgk + stg2k, [K("stg", j)] + stgk)
            for dc in range(nD):
                b = dc % 8
                for j in range(4):
                    PE(lambda e, b=b, j=j, dc=dc: e.matmul(PS[b][:, j * 128:(j + 1) * 128],
                                                           lhsT=stg[:, j, dc * 128:(dc + 1) * 128], rhs=ident_f[:],
                                                           start=True, stop=True),
                       [K("stg", j), K("ident_f")] + stgk, [K("ps", b)])
                ACT(lambda e, b=b, dc=dc, tt=tt: e.activation(out=X[:, dc, ts(tt)], in_=PS[b][:, :], func=AF.Copy),
                    [K("ps", b)], XK(dc, tt))

        import os
        if int(os.environ.get("KSTAGE", "99")) >= 1:
            for _ in ada_start(0):
                pass

        def sumsq_finish(bank0=6):
            for tt in range(NTT):
                b = bank0 + tt
                PE(lambda e, b=b, tt=tt: e.matmul(PS[b][:, :], lhsT=ones_f[:], rhs=acc[:, ts(tt)], start=True, stop=True),
                   [K("ones_f"), K("acc", tt)], [K("ps", b)])
                DVE(lambda e, b=b, tt=tt: e.tensor_scalar(out=rstd[:, ts(tt)], in0=PS[b][:, :], scalar1=float(D * NORM_EPS),
                                                          scalar2=None, op0=ALU.add),
                    [K("ps", b)], [K("acc", tt)])
                ACT(lambda e, tt=tt: e.activation(out=rstd[:, ts(tt)], in_=rstd[:, ts(tt)], func=AF.Ln), [K("acc", tt)], [K("acc", tt)])
                ACT(lambda e, tt=tt: e.activation(out=rstd[:, ts(tt)], in_=rstd[:, ts(tt)], func=AF.Exp, scale=-0.5), [K("acc", tt)], [K("acc", tt)])

        def acc_add(first, src_ap, src_keys, tt):
            if first:
                ACT(lambda e: e.activation(out=acc[:, ts(tt)], in_=src_ap, func=AF.Square), src_keys, [K("acc", tt)])
            else:
                t, tk = tmpf()
                ACT(lambda e: e.activation(out=t, in_=src_ap, func=AF.Square), src_keys, tk)
                DVE(lambda e: e.tensor_tensor(out=acc[:, ts(tt)], in0=acc[:, ts(tt)], in1=t, op=ALU.add),
                    tk + [K("acc", tt)], [K("acc", tt)])

        def prenorm(si):
            par = st["par"]
            for dc in range(nD):
                for tt in range(NTT):
                    acc_add(dc == 0, X[:, dc, ts(tt)], XK(dc, tt), tt)
            sumsq_finish()
            for dc in range(nD):
                for tt in range(NTT):
                    t, tk = tmpf()
                    DVE(lambda e, dc=dc, tt=tt, t=t: e.tensor_tensor(out=t, in0=X[:, dc, ts(tt)], in1=rstd[:, ts(tt)], op=ALU.mult),
                        XK(dc, tt) + [K("acc", tt)], tk)
                    ACT(lambda e, dc=dc, tt=tt, t=t: e.activation(out=U[:, dc, ts(tt)], in_=t, func=AF.Identity,
                                                                  scale=coefA[:, si * nD + dc:si * nD + dc + 1],
                                                                  bias=modt[:, par, (3 * si) * nD + dc:(3 * si) * nD + dc + 1]),
                        tk + [K("coefA"), K("modt", par, 3 * si * nD + dc)], UK(dc, tt))

        def post_update(si):
            sumsq_finish()
            for dc in range(nD):
                for tt in range(NTT):
                    t, tk = tmpf()
                    DVE(lambda e, dc=dc, tt=tt, t=t: e.tensor_tensor(out=t, in0=U[:, dc, ts(tt)], in1=rstd[:, ts(tt)], op=ALU.mult),
                        UK(dc, tt) + [K("acc", tt)], tk)
                    DVE(lambda e, dc=dc, tt=tt, t=t: e.scalar_tensor_tensor(out=X[:, dc, ts(tt)], in0=t,
                                                                            scalar=coefR[:, si * nD + dc:si * nD + dc + 1],
                                                                            in1=X[:, dc, ts(tt)], op0=ALU.mult, op1=ALU.add),
                        tk + [K("coefR")] + XK(dc, tt), XK(dc, tt))

        def out_evac_factory(first_flag):
            def mk(dc0):
                def evac(ci, tt, ps, pk):
                    dc = dc0 + ci
                    acc_add(first_flag["v"] and ci == 0 and dc0 == 0, ps, [pk], tt)
                    DVE(lambda e: e.tensor_copy(out=U[:, dc, ts(tt)], in_=ps), [pk], UK(dc, tt))
                return evac
            return mk

        def ffn(l, fi, si):
            import os
            KSUB = int(os.environ.get("KSUB", "99"))
            prenorm(si)
            if KSUB < 1:
                return
            nF = c.nF
            Hc = [hview(i * SB, [128, T], BF16) for i in range(nF)]
            for c0 in range(0, nF, 2):
                ncol = min(256, (nF - c0) * 128)

                def evac_g(ci, tt, ps, pk, c0=c0):
                    hv, hk = Hc[c0 + ci]
                    ACT(lambda e: e.activation(out=hv[:, ts(tt)], in_=ps, func=AF.Silu), [pk], [hk[tt]])

                def evac_u(ci, tt, ps, pk, c0=c0):
                    hv, hk = Hc[c0 + ci]
                    DVE(lambda e: e.tensor_tensor(out=hv[:, ts(tt)], in0=ps, in1=hv[:, ts(tt)], op=ALU.mult),
                        [pk, hk[tt]], [hk[tt]])
                gemm(wgu[fi], l, D, c0 * 128, ncol, rhsU, evac_g)
                gemm(wgu[fi], l, D, c.F + c0 * 128, ncol, rhsU, evac_u)
            if KSUB < 2:
                return
            ff = {"v": True}
            mk = out_evac_factory(ff)
            for d0 in range(0, nD, 2):
                gemm(wd[fi], l, c.F, d0 * 128, min(256, (nD - d0) * 128), lambda kc: (Hc[kc][0], Hc[kc][1]), mk(d0))
                ff["v"] = False
            if KSUB < 3:
                return
            post_update(si)

        def layer_consts(l):
            par = l % 2
            st["par"] = par
            for (t, d, nm) in ((ngt, ng_d, "ngt"), (convw, convw_d, "convw"), (convb, convb_d, "convb"), (dtb, dtb_d, "dtb"),
                               (aneg, alog_d, "aneg"), (dsk, dsk_d, "dsk"), (sng, sng_d, "sng"), (scw, scw_d, "scw")):
                dma("sp", t[:], d[l, :, :], [], [K(nm)])
            ACT(lambda e: e.activation(out=aneg[:], in_=aneg[:], func=AF.Exp), [K("aneg")], [K("aneg")])
            DVE(lambda e: e.tensor_scalar(out=aneg[:], in0=aneg[:], scalar1=-1.0, scalar2=None, op0=ALU.mult),
                [K("aneg")], [K("aneg")])
            sq = math.sqrt(D)
            for si in range(3):
                DVE(lambda e, si=si: e.scalar_tensor_tensor(out=coefA[:, si * nD:(si + 1) * nD],
                                                            in0=modt[:, par, (3 * si + 1) * nD:(3 * si + 2) * nD], scalar=1.0,
                                                            in1=ngt[:, (2 * si) * nD:(2 * si + 1) * nD], op0=ALU.add, op1=ALU.mult),
                    MODK(par, (3 * si + 1) * nD, nD) + [K("ngt"), K("coefA")], [K("coefA")])
                w = 1.0 if si == 1 else 0.5
                DVE(lambda e, si=si, w=w: e.scalar_tensor_tensor(out=coefR[:, si * nD:(si + 1) * nD],
                                                                 in0=modt[:, par, (3 * si + 2) * nD:(3 * si + 3) * nD], scalar=w * sq,
                                                                 in1=ngt[:, (2 * si + 1) * nD:(2 * si + 2) * nD], op0=ALU.mult, op1=ALU.mult),
                    MODK(par, (3 * si + 2) * nD, nD) + [K("ngt"), K("coefR")], [K("coefR")])
            DVE(lambda e: e.tensor_scalar(out=coefA[:], in0=coefA[:], scalar1=sq, scalar2=None, op0=ALU.mult),
                [K("coefA")], [K("coefA")])

        def mixer(l):
            si = 1
            prenorm(si)
            H, P_, G, N = c.H, c.P, c.G, c.N
            HB = c.HB; nblk = H // HB; bch = HB * P_ // 128
            al = Al(0)
            Yssd = [al.get([128, T], BF16) for _ in range(c.nHP)]
            Mg = [al.get([128, T], BF16) for _ in range(nD)]
            mark = al.off
            al2 = Al(c.nHP)
            dt_t, dt_k = al2.get([128, NTC, 2 * H], F32)
            adt_t, adt_k = al2.get([128, NTC, 2 * H], F32)
            ncs_t, ncs_k = al2.get([128, NTC, 2 * H], F32)
            dd_t, dd_k = al2.get([128, NTC, 2 * H], F32)
            et_t, et_k = al2.get([128, NTC, 2 * H], F32)
            wdt_t, wdt_k = al2.get([128, nD, 2 * H], BF16)
            BT, BTk = al2.get([128, T], BF16)
            CT, CTk = al2.get([128, T], BF16)
            Btm, Btmk = al2.get([128, NTC, 128], BF16)
            xsf = [al2.get([128, T], BF16) for _ in range(bch)]
            Yg = [al2.get([128, T], F32) for _ in range(bch)]
            pp_off = al2.off
            Pp, Ppk = al2.get([128, c.NSEG, 259], F32)
            cacc, cacck = al2.get([128, T], F32)
            pp_end = al2.off
            RD, RDk = al2.get([128, HB, 128], F32)
            Ec, Eck = al2.get([128, HB, 128], F32)
            Mt, Mtk = al2.get([128, HB, 128], BF16)
            Cp, Cpk = al2.get([128, HB, 128], BF16)
            Gm, Gmk = al2.get([128, 128], F32)
            xd, xdk = al2.get([128, HB * P_], BF16)
            xdd, xddk = al2.get([128, HB * P_], BF16)
            need_b = 2 * 2048 * HB // 4 + 2 * 1024 * HB // 4 + 1024 + 2 * max(1024, HB * P_ * 2)
            if pp_end - pp_off >= need_b:
                alB = Al(0); alB.off = pp_off
            else:
                alB = al2; pp_end = NSLOT * SB
            tB = []
            for shp, dty in (([128, HB, 128], F32), ([128, HB, 128], F32), ([128, HB, 128], BF16), ([128, HB, 128], BF16), ([128, 128], F32),
                             ([128, HB * P_], BF16), ([128, HB * P_], BF16)):
                v_, k_ = alB.get(shp, dty)
                tB += [v_, k_]
            assert alB.off <= pp_end, "chain-B temporaries do not fit in conv scratch"
            chains = [{"t": [RD, RDk, Ec, Eck, Mt, Mtk, Cp, Cpk, Gm, Gmk, xd, xdk, xdd, xddk], "banks": (0, 1, 2)},
                      {"t": tB, "banks": (3, 4, 5)}]
            hst = [al2.get([128, HB * P_], F32) for _ in range(2)]
            hbf = [al2.get([128, HB * P_], BF16) for _ in range(2)]
            h0s, h0sk = al2.get([128, N], F32)
            hot, hotk = al2.get([128, bch, N], F32)

            for t0 in range(0, nD, 8):
                pass
            dma("pool", wdt_t[:, :, :], w_in[l, :, c.o_dt:c.o_dt + 2 * H].rearrange("(kc p) n -> p kc n", p=128), [], wdt_k)
            b = 4
            for tc in range(NTC):
                for kc in range(nD):
                    PE(lambda e, tc=tc, kc=kc: e.matmul(PS[b][:, tc * 2 * H:(tc + 1) * 2 * H], lhsT=U[:, kc, tc * 128:(tc + 1) * 128],
                                                        rhs=wdt_t[:, kc, :], start=(kc == 0), stop=(kc == nD - 1)),
                       wdt_k + UK(kc, tc // 4), [K("ps", b)])
            psv = PS[b][:, 0:NTC * 2 * H].rearrange("p (a b) -> p a b", a=NTC)
            DVE(lambda e: e.tensor_tensor(out=dt_t, in0=psv, in1=dtb[:].unsqueeze(1).broadcast_to([128, NTC, 2 * H]), op=ALU.add),
                [K("ps", b), K("dtb")], dt_k)
            ACT(lambda e: e.activation(out=dt_t, in_=dt_t, func=AF.Exp), dt_k, dt_k)
            ACT(lambda e: e.activation(out=dt_t, in_=dt_t, func=AF.Ln, bias=1.0), dt_k, dt_k)
            DVE(lambda e: e.tensor_tensor(out=adt_t, in0=dt_t, in1=aneg[:].unsqueeze(1).broadcast_to([128, NTC, 2 * H]), op=ALU.mult),
                dt_k + [K("aneg")], adt_k)
            W2 = NTC * 2 * H
            adt_flat = adt_t.rearrange("p a b -> p (a b)")
            b0, b1 = 5, 6
            PE(lambda e: e.matmul(PS[b0][:, 0:W2], lhsT=ones_f[:], rhs=adt_flat, start=True, stop=True),
               [K("ones_f")] + adt_k, [K("ps", b0)])
            for tc in range(NTC):
                PE(lambda e, tc=tc: e.matmul(PS[b1][:, tc * 2 * H:tc * 2 * H + H], lhsT=mk_f[:], rhs=adt_t[:, tc, 0:H], start=True, stop=True),
                   [K("mk_f")] + adt_k, [K("ps", b1)])
                PE(lambda e, tc=tc: e.matmul(PS[b1][:, tc * 2 * H + H:(tc + 1) * 2 * H], lhsT=mk_b[:], rhs=adt_t[:, tc, H:2 * H], start=True, stop=True),
                   [K("mk_b")] + adt_k, [K("ps", b1)])
            tot_v = PS[b0][:, 0:W2].rearrange("p (a b) -> p a b", a=NTC)
            cs_v = PS[b1][:, 0:W2].rearrange("p (a b) -> p a b", a=NTC)
            DVE(lambda e: e.tensor_scalar(out=ncs_t, in0=cs_v, scalar1=-1.0, scalar2=None, op0=ALU.mult), [K("ps", b1)], ncs_k)
            DVE(lambda e: e.tensor_tensor(out=dd_t, in0=tot_v, in1=ncs_t, op=ALU.add), [K("ps", b0)] + ncs_k, dd_k)
            ACT(lambda e: e.activation(out=dd_t, in_=dd_t, func=AF.Exp), dd_k, dd_k)
            DVE(lambda e: e.tensor_tensor(out=dd_t, in0=dd_t, in1=dt_t, op=ALU.mult), dd_k + dt_k, dd_k)
            ACT(lambda e: e.activation(out=et_t, in_=tot_v, func=AF.Exp), [K("ps", b0)], et_k)

            spt = 512 // c.SEG

            def conv_from_pp(wt, wkey, wbase, ntap, right):
                NS = c.NSEG
                if NS > 1:
                    DVE(lambda e: e.tensor_scalar(out=Pp[:, 1:NS, 0:1], in0=Pp[:, 0:NS - 1, 256:257], scalar1=keep[:, 0:1], scalar2=None, op0=ALU.mult),
                        Ppk + [K("keep")], Ppk)
                    DVE(lambda e: e.tensor_scalar(out=Pp[:, 0:NS - 1, 257:257 + right], in0=Pp[:, 1:NS, 1:1 + right], scalar1=keep[:, 0:1], scalar2=None, op0=ALU.mult),
                        Ppk + [K("keep")], Ppk)
                cv3 = cacc.rearrange("p (a b) -> p a b", a=NS)
                DVE(lambda e: e.tensor_scalar(out=cv3, in0=Pp[:, :, 0:256], scalar1=wt[:, wbase:wbase + 1], scalar2=None, op0=ALU.mult),
                    Ppk + [K(wkey)], cacck)
                for k in range(1, ntap):
                    DVE(lambda e, k=k: e.scalar_tensor_tensor(out=cv3, in0=Pp[:, :, k:k + 256], scalar=wt[:, wbase + k:wbase + k + 1],
                                                              in1=cv3, op0=ALU.mult, op1=ALU.add),
                        Ppk + cacck + [K(wkey)], cacck)

            DVE(lambda e: e.memset(Pp, 0.0), [], Ppk)

            def xbc_chunk(ch, dst, dstk):
                def evac(ci, tt, ps, pk):
                    ACT(lambda e: e.activation(out=Pp[:, tt * spt:(tt + 1) * spt, 1:257], in_=ps.rearrange("p (a b) -> p a b", a=spt), func=AF.Copy),
                        [pk], Ppk)
                DVE(lambda e: e.memset(Pp[:, 0, 0:1], 0.0), [], Ppk)
                DVE(lambda e: e.memset(Pp[:, c.NSEG - 1, 257:259], 0.0), [], Ppk)
                gemm(w_in, l, D, c.o_xbc + ch * 128, 128, rhsU, evac)
                conv_from_pp(convw, "convw", ch * 4, 4, 2)
                ACT(lambda e: e.activation(out=dst, in_=cacc, func=AF.Silu, bias=convb[:, ch:ch + 1]), cacck + [K("convb")], dstk)

            def ssd_unit(ch, l, blk, hb0, d, ui, tc, nord, ywritten):
                RD, RDk, Ec, Eck, Mt, Mtk, Cp, Cpk, Gm, Gmk, xd, xdk, xdd, xddk = ch["t"]
                bA, bB, bC = ch["banks"]
                mk = mk_f if d == 0 else mk_b
                mkn = "mk_f" if d == 0 else "mk_b"
                hs, hsk = hst[d]; hb_, hbk = hbf[d]
                seg = tc // 2
                seg_first = (tc % 2 == 0) if d == 0 else (tc % 2 == 1)
                seg_last = not seg_first
                tsl = slice(tc * 128, (tc + 1) * 128)
                hsl = slice(d * H + hb0, d * H + hb0 + HB)
                W_ = HB * P_
                if ui == 0:
                    for j in range(bch):
                        r0 = (blk * bch + j) * 128
                        dma("sp", h0s, h0[l, d, r0:r0 + 128, :], [], h0sk)
                        PE(lambda e, j=j: e.matmul(PS[bA][:, 128 + j * 128:128 + (j + 1) * 128], lhsT=h0s, rhs=ident_f[:], start=True, stop=True),
                           h0sk + [K("ident_f")], [K("ps", bA)])
                    ACT(lambda e: e.activation(out=hs, in_=PS[bA][:, 128:128 + W_], func=AF.Copy), [K("ps", bA)], hsk)
                    ACT(lambda e: e.activation(out=hb_, in_=PS[bA][:, 128:128 + W_], func=AF.Copy), [K("ps", bA)], hbk)
                    yield
                elif seg_first:
                    DVE(lambda e: e.tensor_scalar(out=hs, in0=hs, scalar1=keep[:, 0:1], scalar2=None, op0=ALU.mult), hsk + [K("keep")], hsk)
                    ACT(lambda e: e.activation(out=hb_, in_=hs, func=AF.Copy), hsk, hbk)
                    yield
                PE(lambda e: e.matmul(PS[bA][:, 0:128], lhsT=BT[:, tsl], rhs=CT[:, tsl], start=True, stop=True), BTk + CTk, [K("ps", bA)])
                DVE(lambda e: e.tensor_tensor(out=RD, in0=mk[:].unsqueeze(1).broadcast_to([128, HB, 128]),
                                              in1=adt_t[:, tc, hsl].unsqueeze(2).broadcast_to([128, HB, 128]), op=ALU.mult),
                    [K(mkn)] + adt_k, RDk)
                yield
                PE(lambda e: e.matmul(PS[bB][:, 0:HB * 128], lhsT=ones_f[:], rhs=RD.rearrange("p a b -> p (a b)"), start=True, stop=True),
                   [K("ones_f")] + RDk, [K("ps", bB)])
                DVE(lambda e: e.tensor_tensor(out=Gm, in0=PS[bA][:, 0:128], in1=mk[:], op=ALU.mult), [K("ps", bA), K(mkn)], Gmk)
                yield
                csb = PS[bB][:, 0:HB * 128].rearrange("p (a b) -> p a b", a=HB)
                for j in range(bch):
                    PE(lambda e, j=j: e.matmul(PS[bA][:, 128 + j * 128:128 + (j + 1) * 128], lhsT=xsf[j][0][:, tsl], rhs=ident_b[:], start=True, stop=True),
                       xsf[j][1] + [K("ident_b")], [K("ps", bA)])
                DVE(lambda e: e.tensor_tensor(out=RD, in0=csb, in1=ncs_t[:, tc, hsl].unsqueeze(2).broadcast_to([128, HB, 128]), op=ALU.add),
                    [K("ps", bB)] + ncs_k, RDk)
                yield
                ACT(lambda e: e.activation(out=Ec, in_=csb, func=AF.Exp), [K("ps", bB)], Eck)
                DVE(lambda e: e.tensor_scalar(out=RD, in0=RD, scalar1=0.0, scalar2=None, op0=ALU.min), RDk, RDk)
                yield
                ACT(lambda e: e.activation(out=RD, in_=RD, func=AF.Exp), RDk, RDk)
                xps = PS[bA][:, 128:128 + W_].rearrange("p (a b) -> p a b", a=HB)
                DVE(lambda e: e.tensor_tensor(out=xd.rearrange("p (a b) -> p a b", a=HB), in0=xps,
                                              in1=dt_t[:, tc, hsl].unsqueeze(2).broadcast_to([128, HB, P_]), op=ALU.mult),
                    [K("ps", bA)] + dt_k, xdk)
                yield
                DVE(lambda e: e.tensor_tensor(out=xdd.rearrange("p (a b) -> p a b", a=HB), in0=xps,
                                              in1=dd_t[:, tc, hsl].unsqueeze(2).broadcast_to([128, HB, P_]), op=ALU.mult),
                    [K("ps", bA)] + dd_k, xddk)
                yield
                DVE(lambda e: e.tensor_tensor(out=Cp, in0=Ec, in1=CT[:, tsl].unsqueeze(1).broadcast_to([128, HB, 128]), op=ALU.mult),
                    Eck + CTk, Cpk)
                yield
                DVE(lambda e: e.scalar_tensor_tensor(out=Mt, in0=RD, scalar=1.0, in1=Gm.unsqueeze(1).broadcast_to([128, HB, 128]),
                                                     op0=ALU.min, op1=ALU.mult), RDk + Gmk, Mtk)
                yield
                for hh in range(HB):
                    po = (hh % 2) * 64; co = (hh // 2) * 128
                    PE(lambda e, hh=hh, po=po, co=co: e.matmul(PS[bC][po:po + 64, co:co + 128], lhsT=xd[:, hh * P_:(hh + 1) * P_],
                                                               rhs=Mt[:, hh, :], start=True, stop=False),
                       xdk + Mtk, [K("ps", bC)])
                    PE(lambda e, hh=hh, po=po, co=co: e.matmul(PS[bC][po:po + 64, co:co + 128], lhsT=hb_[:, hh * P_:(hh + 1) * P_],
                                                               rhs=Cp[:, hh, :], start=False, stop=True),
                       hbk + Cpk, [K("ps", bC)])
                PE(lambda e: e.matmul(PS[bC][:, 256:256 + W_], lhsT=Btm[:, tc, :], rhs=xdd, start=True, stop=True), Btmk + xddk, [K("ps", bC)])
                yield
                for j in range(bch):
                    yv, yk = Yg[j]
                    ykk = [yk[(tc * 512) // 1024]]
                    if (j, tc) not in ywritten:
                        ywritten.add((j, tc))
                        ACT(lambda e, j=j, yv=yv: e.activation(out=yv[:, tsl], in_=PS[bC][:, j * 128:(j + 1) * 128], func=AF.Copy),
                            [K("ps", bC)], ykk)
                    else:
                        DVE(lambda e, j=j, yv=yv: e.tensor_tensor(out=yv[:, tsl], in0=PS[bC][:, j * 128:(j + 1) * 128], in1=yv[:, tsl], op=ALU.add),
                            [K("ps", bC)] + ykk, ykk)
                yield
                DVE(lambda e: e.tensor_tensor(out=hs.rearrange("p (a b) -> p a b", a=HB), in0=hs.rearrange("p (a b) -> p a b", a=HB),
                                              in1=et_t[:, tc, hsl].unsqueeze(2).broadcast_to([128, HB, P_]), op=ALU.mult),
                    hsk + et_k, hsk)
                yield
                DVE(lambda e: e.tensor_tensor(out=hs, in0=PS[bC][:, 256:256 + W_], in1=hs, op=ALU.add), [K("ps", bC)] + hsk, hsk)
                yield
                if ui != nord - 1:
                    ACT(lambda e: e.activation(out=hb_, in_=hs, func=AF.Copy), hsk, hbk)
                if seg_last:
                    for j in range(bch):
                        PE(lambda e, j=j: e.matmul(PS[bA][:, 128 + j * 128:128 + (j + 1) * 128], lhsT=hs[:, j * 128:(j + 1) * 128], rhs=ident_f[:],
                                                   start=True, stop=True), hsk + [K("ident_f")], [K("ps", bA)])
                    yield
                    ACT(lambda e: e.activation(out=hot, in_=PS[bA][:, 128:128 + bch * 128].rearrange("p (a b) -> p a b", a=bch), func=AF.Copy),
                        [K("ps", bA)], hotk)
                    r0 = blk * bch * 128
                    dma("sp", hout[l, seg, d, r0:r0 + bch * 128, :].rearrange("(a p) n -> p a n", p=128), hot, hotk, [K("hout", l, seg, d, blk)])
                yield

            import os
            MIXSUB = int(os.environ.get("MIXSUB", "7"))
            first_sq = {"v": True}
            for blk in range(nblk if (MIXSUB & 1) else 0):
                g = (blk * HB) // c.hg
                hb0 = blk * HB
                if blk == 0 or g != ((blk - 1) * HB) // c.hg:
                    xbc_chunk(c.nHP + g, BT, BTk)
                    xbc_chunk(c.nHP + G + g, CT, CTk)
                    for tc in range(NTC):
                        bb = 6 + (tc % 2)
                        PE(lambda e, tc=tc, bb=bb: e.matmul(PS[bb][:, 0:128], lhsT=BT[:, tc * 128:(tc + 1) * 128], rhs=ident_b[:], start=True, stop=True),
                           BTk + [K("ident_b")], [K("ps", bb)])
                        ACT(lambda e, tc=tc, bb=bb: e.activation(out=Btm[:, tc, :], in_=PS[bb][:, 0:128], func=AF.Copy), [K("ps", bb)], Btmk)
                for j in range(bch):
                    xbc_chunk(blk * bch + j, xsf[j][0], xsf[j][1])
                ywritten = set()
                if _os.environ.get("KOLD", "0") == "1":
                    for d_ in range(2):
                        for ui in range(NTC):
                            for _ in ssd_unit(chains[1 - d_ if _os.environ.get("KSWAP", "0") == "1" else d_], l, blk, hb0, d_, ui, ui if d_ == 0 else NTC - 1 - ui, NTC, ywritten):
                                pass
                for ui in range(NTC if _os.environ.get("KOLD", "0") != "1" else 0):
                    gens = [ssd_unit(chains[0], l, blk, hb0, 0, ui, ui, NTC, ywritten),
                            ssd_unit(chains[1], l, blk, hb0, 1, ui, NTC - 1 - ui, NTC, ywritten)]
                    live = list(gens)
                    if _os.environ.get("KSEQ", "0") == "1":
                        for gobj in gens:
                            for _ in gobj:
                                pass
                        live = []
                    while live:
                        for gobj in list(live):
                            try:
                                next(gobj)
                            except StopIteration:
                                live.remove(gobj)
                    bg_tick()
                for j in range(bch):
                    ch = blk * bch + j

                    def evac(ci, tt, ps, pk, j=j, ch=ch):
                        yv, yk = Yg[j]
                        t, tk = tmpf()
                        ACT(lambda e: e.activation(out=t, in_=ps, func=AF.Silu), [pk], tk)
                        DVE(lambda e: e.scalar_tensor_tensor(out=yv[:, ts(tt)], in0=xsf[j][0][:, ts(tt)], scalar=dsk[:, ch:ch + 1], in1=yv[:, ts(tt)],
                                                             op0=ALU.mult, op1=ALU.add), xsf[j][1] + yk + [K("dsk")], yk)
                        DVE(lambda e: e.tensor_tensor(out=yv[:, ts(tt)], in0=yv[:, ts(tt)], in1=t, op=ALU.mult), yk + tk, yk)
                        acc_add(first_sq["v"] and ch == 0, yv[:, ts(tt)], yk, tt)
                        ACT(lambda e: e.activation(out=Yssd[ch][0][:, ts(tt)], in_=yv[:, ts(tt)], func=AF.Identity, scale=sng[:, ch:ch + 1]),
                            yk + [K("sng")], [Yssd[ch][1][tt]])
                    gemm(w_in, l, D, c.o_z + ch * 128, 128, rhsU, evac)
                first_sq["v"] = False
            for tt in range(NTT):
                bq = 6 + tt
                PE(lambda e, bq=bq, tt=tt: e.matmul(PS[bq][:, :], lhsT=ones_f[:], rhs=acc[:, ts(tt)], start=True, stop=True),
                   [K("ones_f"), K("acc", tt)], [K("ps", bq)])
                DVE(lambda e, bq=bq, tt=tt: e.tensor_scalar(out=rstd[:, ts(tt)], in0=PS[bq][:, :], scalar1=float(1.0 / c.HP), scalar2=float(NORM_EPS),
                                                            op0=ALU.mult, op1=ALU.add), [K("ps", bq)], [K("acc", tt)])
                ACT(lambda e, tt=tt: e.activation(out=rstd[:, ts(tt)], in_=rstd[:, ts(tt)], func=AF.Ln), [K("acc", tt)], [K("acc", tt)])
                ACT(lambda e, tt=tt: e.activation(out=rstd[:, ts(tt)], in_=rstd[:, ts(tt)], func=AF.Exp, scale=-0.5), [K("acc", tt)], [K("acc", tt)])

            al3 = Al(c.nHP + nD)
            SG = [al3.get([128, T], BF16) for _ in range(2)]

            mstate = {"first": True}

            def branch_merge(bi, Ysrc, Kdim, use_rstd):
                is_first = mstate["first"]; mstate["first"] = False
                for d0 in range(0, nD, 2):
                    def evac_gate(ci, tt, ps, pk):
                        sv, sk = SG[ci]
                        ACT(lambda e: e.activation(out=sv[:, ts(tt)], in_=ps, func=AF.Sigmoid), [pk], [sk[tt]])
                        if use_rstd:
                            DVE(lambda e: e.tensor_tensor(out=sv[:, ts(tt)], in0=sv[:, ts(tt)], in1=rstd[:, ts(tt)], op=ALU.mult),
                                [sk[tt], K("acc", tt)], [sk[tt]])

                    def evac_br(ci, tt, ps, pk, d0=d0):
                        sv, sk = SG[ci]
                        mv, mkk = Mg[d0 + ci]
                        if is_first:
                            DVE(lambda e: e.tensor_tensor(out=mv[:, ts(tt)], in0=ps, in1=sv[:, ts(tt)], op=ALU.mult), [pk, sk[tt]], [mkk[tt]])
                        else:
                            t, tk = tmpf()
                            DVE(lambda e: e.tensor_tensor(out=t, in0=ps, in1=sv[:, ts(tt)], op=ALU.mult), [pk, sk[tt]], tk)
                            DVE(lambda e: e.tensor_tensor(out=mv[:, ts(tt)], in0=mv[:, ts(tt)], in1=t, op=ALU.add), tk + [mkk[tt]], [mkk[tt]])
                    ncol = min(256, (nD - d0) * 128)
                    gemm(w_in, l, D, c.o_gate + bi * D + d0 * 128, ncol, rhsU, evac_gate)
                    gemm(w_br[bi], l, Kdim, d0 * 128, ncol, lambda kc: (Ysrc[kc][0], Ysrc[kc][1]), evac_br)
                    bg_tick()

            if MIXSUB & 1:
                branch_merge(0, Yssd, c.HP, True)

            al4 = Al(0)
            Ysc = [al4.get([128, T], BF16) for _ in range(c.nSC)]
            al5 = Al(c.nHP + nD + 2)
            t1, t1k = al5.get([128, T], F32)
            Pp2, Pp2k = al5.get([128, c.NSEG, 259], F32)
            cacc2, cacc2k = al5.get([128, T], F32)
            DVE(lambda e: e.memset(Pp2, 0.0), [], Pp2k)
            NS = c.NSEG
            for ch in range(c.nSC if (MIXSUB & 2) else 0):
                def ev_c(ci, tt, ps, pk):
                    ACT(lambda e: e.activation(out=t1[:, ts(tt)], in_=ps, func=AF.Copy), [pk], t1k)

                def ev_x(ci, tt, ps, pk):
                    DVE(lambda e: e.tensor_tensor(out=Pp2[:, tt * spt:(tt + 1) * spt, 1:257], in0=ps.rearrange("p (a b) -> p a b", a=spt),
                                                  in1=t1[:, ts(tt)].rearrange("p (a b) -> p a b", a=spt), op=ALU.mult), [pk] + t1k, Pp2k)
                gemm(w_in, l, D, c.o_scc + ch * 128, 128, rhsU, ev_c)
                gemm(w_in, l, D, c.o_scx + ch * 128, 128, rhsU, ev_x)
                if NS > 1:
                    DVE(lambda e: e.tensor_scalar(out=Pp2[:, 1:NS, 0:1], in0=Pp2[:, 0:NS - 1, 256:257], scalar1=keep[:, 0:1], scalar2=None, op0=ALU.mult),
                        Pp2k + [K("keep")], Pp2k)
                    DVE(lambda e: e.tensor_scalar(out=Pp2[:, 0:NS - 1, 257:258], in0=Pp2[:, 1:NS, 1:2], scalar1=keep[:, 0:1], scalar2=None, op0=ALU.mult),
                        Pp2k + [K("keep")], Pp2k)
                cv3 = cacc2.rearrange("p (a b) -> p a b", a=NS)
                DVE(lambda e, ch=ch: e.tensor_scalar(out=cv3, in0=Pp2[:, :, 0:256], scalar1=scw[:, ch * 3:ch * 3 + 1], scalar2=None, op0=ALU.mult),
                    Pp2k + [K("scw")], cacc2k)
                for k in range(1, 3):
                    DVE(lambda e, k=k, ch=ch: e.scalar_tensor_tensor(out=cv3, in0=Pp2[:, :, k:k + 256], scalar=scw[:, ch * 3 + k:ch * 3 + k + 1], in1=cv3,
                                                                     op0=ALU.mult, op1=ALU.add), Pp2k + cacc2k + [K("scw")], cacc2k)

                def ev_b(ci, tt, ps, pk, ch=ch):
                    DVE(lambda e: e.tensor_tensor(out=Ysc[ch][0][:, ts(tt)], in0=ps, in1=cacc2[:, ts(tt)], op=ALU.mult), [pk] + cacc2k, [Ysc[ch][1][tt]])
                gemm(w_in, l, D, c.o_scb + ch * 128, 128, rhsU, ev_b)
            if MIXSUB & 2:
                branch_merge(1, Ysc, c.SCW, False)

            al6 = Al(0)
            Yft = [al6.get([128, T], BF16) for _ in range(c.nFT)]
            al7 = Al(c.nHP + nD + 2)
            Ftc = [(al6 if c.nFT + 2 <= c.nHP else al7).get([128, T], BF16) for _ in range(2)]
            AB, ABk = al7.get([128, NTC, 512], BF16)
            clr = [al7.get([128, T], BF16) for _ in range(2)]
            slr = [al7.get([128, T], BF16) for _ in range(2)]
            for q in range(c.FTG if (MIXSUB & 4) else 0):
                for j in range(2):
                    def ev_f(ci, tt, ps, pk, j=j):
                        ACT(lambda e: e.activation(out=Ftc[j][0][:, ts(tt)], in_=ps, func=AF.Copy), [pk], [Ftc[j][1][tt]])
                    gemm(w_in, l, D, c.o_ft + (q * 2 + j) * 128, 128, rhsU, ev_f)
                for tc in range(NTC):
                    bb = 4 + (tc % 2)
                    for j in range(2):
                        PE(lambda e, tc=tc, j=j, bb=bb: e.matmul(PS[bb][:, :], lhsT=Ftc[j][0][:, tc * 128:(tc + 1) * 128], rhs=cst[:, j, :],
                                                                 start=(j == 0), stop=(j == 1)), Ftc[j][1] + [K("cst")], [K("ps", bb)])
                    ACT(lambda e, tc=tc, bb=bb: e.activation(out=AB[:, tc, :], in_=PS[bb][:, :], func=AF.Copy), [K("ps", bb)], ABk)
                for tc in range(NTC):
                    r = tc % 2
                    dma("pool", clr[r][0], cl_d[tc * 128:(tc + 1) * 128, :], [], clr[r][1])
                    dma("pool", slr[r][0], sl_d[tc * 128:(tc + 1) * 128, :], [], slr[r][1])
                    for jj in range(2):
                        for tt in range(NTT):
                            bb = jj * NTT + tt
                            PE(lambda e, tc=tc, jj=jj, tt=tt, bb=bb, r=r: e.matmul(PS[bb][:, :], lhsT=AB[:, tc, jj * 128:(jj + 1) * 128], rhs=clr[r][0][:, ts(tt)],
                                                                                   start=(tc == 0), stop=False), ABk + clr[r][1], [K("ps", bb)])
                            PE(lambda e, tc=tc, jj=jj, tt=tt, bb=bb, r=r: e.matmul(PS[bb][:, :], lhsT=AB[:, tc, 256 + jj * 128:256 + (jj + 1) * 128], rhs=slr[r][0][:, ts(tt)],
                                                                                   start=False, stop=(tc == NTC - 1)), ABk + slr[r][1], [K("ps", bb)])
                for jj in range(2):
                    for tt in range(NTT):
                        bb = jj * NTT + tt
                        yv, yk = Yft[q * 2 + jj]
                        ACT(lambda e, bb=bb, yv=yv, tt=tt: e.activation(out=yv[:, ts(tt)], in_=PS[bb][:, :], func=AF.Copy), [K("ps", bb)], [yk[tt]])
            if MIXSUB & 4:
                branch_merge(2, Yft, c.FTW, False)

            ff = {"v": True}
            mk2 = out_evac_factory(ff)
            for d0 in range(0, nD, 2):
                gemm(w_out, l, D, d0 * 128, min(256, (nD - d0) * 128), lambda kc: (Mg[kc][0], Mg[kc][1]), mk2(d0))
                ff["v"] = False
            post_update(si)

        import os
        STAGE = int(os.environ.get("KSTAGE", "99"))
        for l in range(L if STAGE >= 2 else 0):
            layer_consts(l)
            if l + 1 < L:
                st["bg"] = ada_start(l + 1)
            if STAGE >= 3:
                ffn(l, 0, 0)
            if STAGE >= 4:
                mixer(l)
            if STAGE >= 5:
                ffn(l, 1, 2)
            if st["bg"] is not None:
                for _ in st["bg"]:
                    pass
                st["bg"] = None

        ostg, ostgk = hview(0, [128, 2, D], F32)
        for tc in range(NTC):
            r = tc % 2
            for d4 in range(0, nD, 4):
                b = (d4 // 4) % 8
                for j in range(min(4, nD - d4)):
                    dc = d4 + j
                    PE(lambda e, b=b, j=j, dc=dc, tc=tc: e.matmul(PS[b][:, j * 128:(j + 1) * 128], lhsT=X[:, dc, tc * 128:(tc + 1) * 128], rhs=ident_f[:],
                                                                  start=True, stop=True), XK(dc, tc // 4) + [K("ident_f")], [K("ps", b)])
                w = min(4, nD - d4) * 128
                ACT(lambda e, b=b, r=r, d4=d4, w=w: e.activation(out=ostg[:, r, d4 * 128:d4 * 128 + w], in_=PS[b][:, 0:w], func=AF.Copy),
                    [K("ps", b)], [K("ostg", r)] + ostgk)
            dma("sp", yout[tc * 128:(tc + 1) * 128, :], ostg[:, r, :], [K("ostg", r)] + ostgk, [K("yout", tc)])
        outk = [k for k in pg.lastw if k[0] in ("yout", "hout")]
        pg.add("sp", lambda e: None, outk, [])

        pg.emit(nc, sems, dsems, block)
    return nc


def _fm(v, n=128):
    sh = v.shape
    return np.ascontiguousarray(np.swapaxes(v.reshape(sh[:-1] + (sh[-1] // n, n)), -1, -2))


def _dft_tables(T, seqlen, ftgd):
    t = np.arange(T)
    blk = (t[:, None] // seqlen) == (t[None, :] // seqlen)
    ang = 2.0 * np.pi * ((t[:, None] % seqlen) * (t[None, :] % seqlen) % seqlen) / seqlen
    s = 1.0 / math.sqrt(seqlen)
    cl = np.where(blk, np.cos(ang) * s, 0.0).astype(np.float32)
    sl = np.where(blk, -np.sin(ang) * s, 0.0).astype(np.float32)
    k = np.arange(ftgd)
    a2 = 2.0 * np.pi * ((k[:, None] * k[None, :]) % ftgd) / ftgd
    s2 = 1.0 / math.sqrt(ftgd)
    cs = np.concatenate([np.cos(a2) * s2, np.sin(a2) * s2], axis=1).astype(np.float32)
    return cl, sl, cs.reshape(2, 128, 512)


def _pos_emb(n_tok, D, grid_w=64, base=10000.0):
    t = np.arange(n_tok)
    r = (t // grid_w).astype(np.float32)[:, None]
    col = (t % grid_w).astype(np.float32)[:, None]
    nf = D // 4
    omega = (1.0 / (np.float32(base) ** (np.arange(nf, dtype=np.float32) / np.float32(nf)))).astype(np.float32)
    return np.concatenate([np.sin(r * omega), np.cos(r * omega), np.sin(col * omega), np.cos(col * omega)], axis=-1).astype(np.float32)


def run(cfg, inp, n_prompt_cores, n_sample_cores, trace=False):
    c = cfg
    f = lambda a: np.ascontiguousarray(np.asarray(a, dtype=np.float32))
    L, D, T = c.DEPTH, c.D, c.T
    xp = f(inp["x_prompt"]); xs = f(inp["x_sample"]); st_ = f(inp["state_ssd"])
    spc = T // c.SEG
    shared = {
        "ada_w": f(inp["ada_w"]),
        "ada_b": _fm(f(inp["ada_b"]).reshape(L, 9, D)).transpose(0, 2, 1, 3).reshape(L, 128, 9 * c.nD).copy(),
        "norm_g": _fm(f(inp["norm_g"])).transpose(0, 2, 1, 3).reshape(L, 128, 6 * c.nD).copy(),
        "ffn1_wgu": f(inp["ffn1_wgu"]), "ffn2_wgu": f(inp["ffn2_wgu"]), "ffn1_wd": f(inp["ffn1_wd"]), "ffn2_wd": f(inp["ffn2_wd"]),
        "w_in": f(inp["w_in"]), "w_br_ssd": f(inp["w_br_ssd"]), "w_br_sc": f(inp["w_br_sc"]), "w_br_ft": f(inp["w_br_ft"]),
        "w_out": f(inp["w_out"]),
        "conv_w": _fm(f(inp["ssd_conv_w"])).transpose(0, 2, 3, 1).reshape(L, 128, c.nXBC * 4).copy(),
        "conv_b": _fm(f(inp["ssd_conv_b"])),
        "dtb": np.broadcast_to(f(inp["ssd_dt_bias"]).reshape(L, 1, 2 * c.H), (L, 128, 2 * c.H)).copy(),
        "alog": np.broadcast_to(f(inp["ssd_a_log"]).reshape(L, 1, 2 * c.H), (L, 128, 2 * c.H)).copy(),
        "dsk": _fm(np.repeat(f(inp["ssd_d"]), c.P, axis=-1)),
        "sng": _fm(f(inp["ssd_norm_g"])),
        "scw": _fm(f(inp["sc_conv_w"])).transpose(0, 2, 3, 1).reshape(L, 128, c.nSC * 3).copy(),
    }
    cl_p, sl_p, cs = _dft_tables(T, c.SEG, c.FTGD)
    cl_s, sl_s, _ = _dft_tables(T, T, c.FTGD)
    pe = _pos_emb(T, D)
    zeros_pos = np.zeros((T, D), np.float32)
    zeros_h0 = np.zeros((L, 2, c.HP, c.N), np.float32)
    in_maps = []
    for ci in range(n_prompt_cores):
        m = dict(shared)
        m["xin"] = xp[ci * spc:(ci + 1) * spc].reshape(T, D)
        m["pos"] = zeros_pos
        m["cv"] = _fm(f(inp["c_ctx"]))
        m["keep"] = np.zeros((128, 1), np.float32)
        m["h0"] = zeros_h0
        m["dft_cs"] = cs; m["dft_cl"] = cl_p; m["dft_sl"] = sl_p
        in_maps.append(m)
    for b in range(n_sample_cores):
        m = dict(shared)
        m["xin"] = xs[b]
        m["pos"] = pe
        m["cv"] = _fm(f(inp["c"])[b])
        m["keep"] = np.ones((128, 1), np.float32)
        m["h0"] = np.ascontiguousarray(st_[b].reshape(L, 2, c.HP, c.N))
        m["dft_cs"] = cs; m["dft_cl"] = cl_s; m["dft_sl"] = sl_s
        in_maps.append(m)
    nc = build(c)
    ncores = n_prompt_cores + n_sample_cores
    res = run_bass_kernel_spmd(nc, in_maps, core_ids=list(range(ncores)), trace=trace)
    outs = res.results
    y_prompt = np.concatenate([outs[ci]["yout"].reshape(spc, c.SEG, D) for ci in range(n_prompt_cores)], axis=0)
    y_sample = np.stack([outs[n_prompt_cores + b]["yout"] for b in range(n_sample_cores)], axis=0)
    ns = np.concatenate([np.transpose(outs[ci]["hout"], (1, 0, 2, 3, 4)) for ci in range(n_prompt_cores)], axis=0)
    new_state = ns.reshape(n_prompt_cores * spc, L, 2, c.H, c.P, c.N)
    return (y_prompt.astype(np.float32), y_sample.astype(np.float32), np.ascontiguousarray(new_state.astype(np.float32))), res


def kernel(**inputs):
    cfg = Cfg()
    outs, _ = run(cfg, inputs, 4, 2)
    return outs
```
